# Optimizing a Trainium2 kernel written in Bass

```python
import jax, jax.numpy as jnp
from jax import lax
import numpy as np

D_MODEL = 2048
BATCH = 16
SEQ = 2048
DEPTH = 4

GRID_W = 64
EPS = 1e-6
D_FF = 5632

D_ATTN = D_MODEL // 2
D_SSD = D_MODEL - D_ATTN
D_MIX = D_ATTN + D_SSD

HEAD_DIM = 128
N_Q_HEADS = D_ATTN // HEAD_DIM
N_KV_HEADS = 2
Q_PER_KV = N_Q_HEADS // N_KV_HEADS
D_KV = N_KV_HEADS * HEAD_DIM
ROPE_AXIS_DIM = HEAD_DIM // 2
ROPE_THETA = 10000.0
Q_BLOCK = 128

SSD_HEAD_DIM = 64
N_SSD_HEADS = D_SSD // SSD_HEAD_DIM
N_GROUPS = 2
HEADS_PER_GROUP = N_SSD_HEADS // N_GROUPS
D_STATE = 128
D_CONV = 5
CHUNK = 128
CONV_DIM = D_SSD + 2 * N_GROUPS * D_STATE

D_IN_PROJ = D_ATTN + 2 * D_KV + D_SSD + CONV_DIM + 2 * N_SSD_HEADS
SPLIT_POINTS = (D_ATTN, D_ATTN + D_KV, D_ATTN + 2 * D_KV, D_ATTN + 2 * D_KV + D_SSD,
                D_ATTN + 2 * D_KV + D_SSD + CONV_DIM)

kernel_name = "hymba_macaron_ssd_axial_gqa_encoder"


def rmsnorm(x, g):
    xf = x.astype(jnp.float32)
    r = lax.rsqrt(jnp.mean(xf * xf, axis=-1, keepdims=True) + EPS)
    return (xf * r * g.astype(jnp.float32)).astype(x.dtype)


def swiglu(x, w_gu, w_down):
    g, u = jnp.split(x @ w_gu, 2, axis=-1)
    return (jax.nn.silu(g) * u) @ w_down


def axial_rope_tables(L):
    rows = L // GRID_W
    row_pos = jnp.repeat(jnp.arange(rows), GRID_W).astype(jnp.float32)
    col_pos = jnp.tile(jnp.arange(GRID_W), rows).astype(jnp.float32)
    inv_freq = ROPE_THETA ** (-jnp.arange(0, ROPE_AXIS_DIM, 2, dtype=jnp.float32) / ROPE_AXIS_DIM)
    ang_r = row_pos[:, None] * inv_freq
    ang_c = col_pos[:, None] * inv_freq
    return (jnp.cos(ang_r), jnp.sin(ang_r), jnp.cos(ang_c), jnp.sin(ang_c))


def _rotate(x, cos, sin):
    shape = (cos.shape[0],) + (1,) * (x.ndim - 3) + (cos.shape[-1],)
    cos, sin = cos.reshape(shape), sin.reshape(shape)
    x1, x2 = jnp.split(x, 2, axis=-1)
    return jnp.concatenate([x1 * cos - x2 * sin, x2 * cos + x1 * sin], axis=-1)


def apply_axial_rope(x, rope):
    cos_r, sin_r, cos_c, sin_c = rope
    xf = x.astype(jnp.float32)
    x_row, x_col = jnp.split(xf, 2, axis=-1)
    out = jnp.concatenate([_rotate(x_row, cos_r, sin_r), _rotate(x_col, cos_c, sin_c)], axis=-1)
    return out.astype(x.dtype)


def attention_group(q, k, v, q_gain, k_gain, out_gain, rope):
    b, L, _ = q.shape
    q = q.reshape(b, L, N_KV_HEADS, Q_PER_KV, HEAD_DIM)
    k = k.reshape(b, L, N_KV_HEADS, HEAD_DIM)
    v = v.reshape(b, L, N_KV_HEADS, HEAD_DIM)
    q = apply_axial_rope(rmsnorm(q, q_gain), rope)
    k = apply_axial_rope(rmsnorm(k, k_gain), rope)
    scale = HEAD_DIM ** -0.5
    nb = L // Q_BLOCK
    qb = q.reshape(b, nb, Q_BLOCK, N_KV_HEADS, Q_PER_KV, HEAD_DIM).transpose(1, 0, 2, 3, 4, 5)

    def one_block(q_blk):
        s = jnp.einsum("bqkrd,bskd->bkrqs", q_blk, k).astype(jnp.float32) * scale
        p = jax.nn.softmax(s, axis=-1)
        return jnp.einsum("bkrqs,bskd->bqkrd", p.astype(v.dtype), v)

    o = lax.map(one_block, qb)
    o = o.transpose(1, 0, 2, 3, 4, 5).reshape(b, L, D_ATTN)
    return rmsnorm(o, out_gain)


def segsum(a):
    T = a.shape[-1]
    cs = jnp.cumsum(a, axis=-1)
    diff = cs[..., :, None] - cs[..., None, :]
    mask = jnp.tril(jnp.ones((T, T), dtype=bool))
    return jnp.where(mask, diff, -jnp.inf)


def ssd_chunked(X, dtA, B, C):
    b, L, G, R, P = X.shape
    nc = L // CHUNK
    X = X.reshape(b, nc, CHUNK, G, R, P)
    B = B.reshape(b, nc, CHUNK, G, D_STATE)
    C = C.reshape(b, nc, CHUNK, G, D_STATE)
    A = dtA.reshape(b, nc, CHUNK, G, R).transpose(0, 3, 4, 1, 2)
    A_cs = jnp.cumsum(A, axis=-1)
    Lmat = jnp.exp(segsum(A))
    CB = jnp.einsum("bclgn,bcsgn->bgcls", C, B)
    y_diag = jnp.einsum("bgcls,bgrcls,bcsgrp->bclgrp", CB, Lmat, X)
    decay_states = jnp.exp(A_cs[..., -1:] - A_cs)
    states = jnp.einsum("bclgn,bgrcl,bclgrp->bcgrpn", B, decay_states, X)
    states = jnp.concatenate([jnp.zeros_like(states[:, :1]), states], axis=1)
    chunk_decay = jnp.exp(segsum(jnp.pad(A_cs[..., -1], ((0, 0),) * 3 + ((1, 0),))))
    states = jnp.einsum("bgrzc,bcgrpn->bzgrpn", chunk_decay, states)[:, :-1]
    y_off = jnp.einsum("bclgn,bcgrpn,bgrcl->bclgrp", C, states, jnp.exp(A_cs))
    return (y_diag + y_off).reshape(b, L, G, R, P)


def centred_depthwise_conv(u, w, bias):
    pad = (D_CONV - 1) // 2
    out = lax.conv_general_dilated(u, w[:, None, :].astype(u.dtype), window_strides=(1,),
                                   padding=[(pad, pad)], dimension_numbers=("NWC", "WIO", "NWC"),
                                   feature_group_count=u.shape[-1])
    return out + bias


def ssd_group(z, xBC, dt_raw, conv_w, conv_b, dt_bias, a_log, d_skip, norm_gain):
    b, L, _ = z.shape
    xBC = jax.nn.silu(centred_depthwise_conv(xBC, conv_w, conv_b)).astype(jnp.float32)
    xs, Bm, Cm = jnp.split(xBC, [D_SSD, D_SSD + N_GROUPS * D_STATE], axis=-1)
    xs = xs.reshape(b, L, N_GROUPS, HEADS_PER_GROUP, SSD_HEAD_DIM)
    Bm = Bm.reshape(b, L, N_GROUPS, D_STATE)
    Cm = Cm.reshape(b, L, N_GROUPS, D_STATE)
    dt = jax.nn.softplus(dt_raw.astype(jnp.float32).reshape(b, L, 2, N_SSD_HEADS)
                         + dt_bias.astype(jnp.float32))
    A = -jnp.exp(a_log.astype(jnp.float32))
    dtA = (dt * A).reshape(b, L, 2, N_GROUPS, HEADS_PER_GROUP)
    dt = dt.reshape(b, L, 2, N_GROUPS, HEADS_PER_GROUP)
    y_fwd = ssd_chunked(xs * dt[:, :, 0, ..., None], dtA[:, :, 0], Bm, Cm)
    flip = lambda t: jnp.flip(t, axis=1)
    y_bwd = flip(ssd_chunked(flip(xs * dt[:, :, 1, ..., None]), flip(dtA[:, :, 1]), flip(Bm), flip(Cm)))
    d = d_skip.astype(jnp.float32).reshape(N_GROUPS, HEADS_PER_GROUP)[..., None]
    y = (y_fwd + y_bwd + d * xs).reshape(b, L, N_GROUPS, D_SSD // N_GROUPS)
    y = y * jax.nn.silu(z.astype(jnp.float32)).reshape(b, L, N_GROUPS, D_SSD // N_GROUPS)
    y = y * lax.rsqrt(jnp.mean(y * y, axis=-1, keepdims=True) + EPS)
    return (y.reshape(b, L, D_SSD) * norm_gain.astype(jnp.float32)).astype(z.dtype)


def setup_inputs(seed: int = 0) -> dict:
    key = jax.random.key(seed)
    ks = jax.random.split(key, 24)
    f32 = jnp.float32
    nrm = lambda k, shape, s: jax.random.normal(k, shape, f32) * s
    gain = lambda k, shape: 1.0 + 0.02 * jax.random.normal(k, shape, f32)
    dt0 = jnp.exp(jax.random.uniform(ks[9], (DEPTH, 2, N_SSD_HEADS), f32,
                                     minval=math_log(1e-3), maxval=math_log(1e-1)))
    return {
        "x": jax.random.normal(ks[0], (BATCH, SEQ, D_MODEL), f32),
        "ffn1_norm": gain(ks[1], (DEPTH, D_MODEL)),
        "ffn1_w_gu": nrm(ks[2], (DEPTH, D_MODEL, 2 * D_FF), D_MODEL ** -0.5),
        "ffn1_w_down": nrm(ks[3], (DEPTH, D_FF, D_MODEL), D_FF ** -0.5),
        "mix_norm": gain(ks[4], (DEPTH, D_MODEL)),
        "w_in": nrm(ks[5], (DEPTH, D_MODEL, D_IN_PROJ), D_MODEL ** -0.5),
        "conv_w": nrm(ks[6], (DEPTH, D_CONV, CONV_DIM), D_CONV ** -0.5),
        "conv_b": nrm(ks[7], (DEPTH, CONV_DIM), 0.02),
        "dt_bias": dt0 + jnp.log(-jnp.expm1(-dt0)),
        "a_log": jnp.log(jax.random.uniform(ks[8], (DEPTH, 2, N_SSD_HEADS), f32, minval=1.0, maxval=16.0)),
        "d_skip": 1.0 + 0.1 * jax.random.normal(ks[10], (DEPTH, N_SSD_HEADS), f32),
        "q_norm": gain(ks[11], (DEPTH, HEAD_DIM)),
        "k_norm": gain(ks[12], (DEPTH, HEAD_DIM)),
        "attn_out_norm": gain(ks[13], (DEPTH, D_ATTN)),
        "ssd_out_norm": gain(ks[14], (DEPTH, D_SSD)),
        "w_out": nrm(ks[15], (DEPTH, D_MIX, D_MODEL), D_MIX ** -0.5),
        "ffn2_norm": gain(ks[16], (DEPTH, D_MODEL)),
        "ffn2_w_gu": nrm(ks[17], (DEPTH, D_MODEL, 2 * D_FF), D_MODEL ** -0.5),
        "ffn2_w_down": nrm(ks[18], (DEPTH, D_FF, D_MODEL), D_FF ** -0.5),
        "final_norm": gain(ks[19], (D_MODEL,)),
    }


def math_log(v):
    return float(np.log(v))


def reference(x, ffn1_norm, ffn1_w_gu, ffn1_w_down, mix_norm, w_in, conv_w, conv_b, dt_bias, a_log,
              d_skip, q_norm, k_norm, attn_out_norm, ssd_out_norm, w_out, ffn2_norm, ffn2_w_gu,
              ffn2_w_down, final_norm):
    L = x.shape[1]
    rope = axial_rope_tables(L)
    h = x
    for i in range(DEPTH):
        h = h + 0.5 * swiglu(rmsnorm(h, ffn1_norm[i]), ffn1_w_gu[i], ffn1_w_down[i])
        u = rmsnorm(h, mix_norm[i])
        q, k, v, z, xBC, dt_raw = jnp.split(u @ w_in[i], SPLIT_POINTS, axis=-1)
        a_out = attention_group(q, k, v, q_norm[i], k_norm[i], attn_out_norm[i], rope)
        s_out = ssd_group(z, xBC, dt_raw, conv_w[i], conv_b[i], dt_bias[i], a_log[i], d_skip[i],
                          ssd_out_norm[i])
        mixed = jnp.concatenate([a_out.astype(h.dtype), s_out.astype(h.dtype)], axis=-1)
        h = h + mixed @ w_out[i]
        h = h + 0.5 * swiglu(rmsnorm(h, ffn2_norm[i]), ffn2_w_gu[i], ffn2_w_down[i])
    return rmsnorm(h, final_norm).astype(x.dtype)
```

```python
import numpy as np
import concourse.bass as bass
import concourse.mybir as mybir
from concourse.bass_utils import run_bass_kernel_spmd
from contextlib import ExitStack

F32 = mybir.dt.float32
BF16 = mybir.dt.bfloat16
AF = mybir.ActivationFunctionType
ALU = mybir.AluOpType
AX = mybir.AxisListType

ENGS = ("pe", "act", "dve", "pool", "sp")
NCORES = 8
D = 2048
L = 2048
DFF = 5632
NJ = 44
DEPTH = 4
EPS = 1e-6
NPC_L = 130
NPR_L = 1360
WIN_BLOCKS = ([("q%d" % i, i * 128, 128) for i in range(8)] + [("k%d" % i, 1024 + i * 128, 128) for i in range(2)]
              + [("v", 1280, 256)] + [("z%d" % i, 1536 + i * 256, 256) for i in range(4)]
              + [("x%d" % i, 2560 + i * 128, 128) for i in range(12)] + [("dt", 4096, 32)])
WIN_OFF = {}
_o = 0
for _n, _c0, _nc in WIN_BLOCKS:
    WIN_OFF[_n] = (_o, _nc)
    _o += 16 * _nc
WIN_TOT = _o


class Op:
    __slots__ = ("eng", "fn", "deps", "signaled", "sem", "val", "is_dma")

    def __init__(self, eng, fn, is_dma):
        self.eng = eng
        self.fn = fn
        self.deps = []
        self.signaled = False
        self.sem = None
        self.val = 0
        self.is_dma = is_dma


class Prog:
    def __init__(self, nc, es, self_sync=True):
        self.nc = nc
        self.es = es
        self.ops = {e: [] for e in ENGS}
        self.last_write = {}
        self.readers = {}
        self.dma_last = {}
        self.dma_cnt = {}
        self.dma_sems = {}
        self.eng_sems = {}
        self.self_sync = self_sync
        self.pending = {}
        self.nops = 0

    def sb(self, name, shape, dt):
        return self.es.enter_context(self.nc.sbuf_tensor("sb_" + name, list(shape), dt))

    def ps(self, name, shape, dt):
        return self.es.enter_context(self.nc.psum_tensor(name, list(shape), dt))

    def _deps(self, op, reads, writes):
        deps = []
        lw = self.last_write
        rd = self.readers
        for r in reads:
            o = lw.get(r)
            if o is not None:
                deps.append(o)
        for w in writes:
            o = lw.get(w)
            if o is not None:
                deps.append(o)
            l = rd.get(w)
            if l:
                deps.extend(l)
        for w in writes:
            lw[w] = op
            rd[w] = []
        for r in reads:
            if isinstance(r, str) and r.startswith("c:"):
                continue
            l = rd.get(r)
            if l is None:
                rd[r] = [op]
            else:
                l.append(op)
        if not (op.is_dma and op.eng == "pool"):
            p = self.pending.pop(op.eng, None)
            if p:
                deps.extend(p)
        return deps

    def barrier(self):
        lst = []
        for e in ("pe", "act", "dve"):
            if self.ops[e]:
                lst.append(self.ops[e][-1])
        for o in reversed(self.ops["pool"]):
            if not o.is_dma:
                lst.append(o)
                break
        for k, o in self.dma_last.items():
            if not str(k).startswith("w:"):
                lst.append(o)
        for e in ("pe", "act", "dve", "sp", "pool"):
            self.pending[e] = list(lst)
        self.last_write = {k: v for k, v in self.last_write.items() if _persist(k)}
        self.readers = {k: v for k, v in self.readers.items() if _persist(k)}

    def add(self, eng, fn, reads=(), writes=()):
        op = Op(eng, fn, False)
        deps = self._deps(op, reads, writes)
        seen = set()
        for d in deps:
            if d is op or id(d) in seen:
                continue
            seen.add(id(d))
            if d.eng == eng and not d.is_dma:
                if eng == "pe" or not self.self_sync:
                    continue
            op.deps.append(d)
            d.signaled = True
        self.ops[eng].append(op)
        self.nops += 1
        return op

    def dma(self, queue, key, out, in_, reads=(), writes=(), **kw):
        def fn(e):
            return e.dma_start(out=out, in_=in_, **kw)
        op = Op(queue, fn, True)
        deps = self._deps(op, reads, writes)
        prev = self.dma_last.get(key)
        if prev is not None:
            deps.append(prev)
        seen = set()
        for d in deps:
            if d is op or id(d) in seen:
                continue
            seen.add(id(d))
            op.deps.append(d)
            d.signaled = True
        self.dma_last[key] = op
        n = self.dma_cnt.get(key, 0) + 1
        self.dma_cnt[key] = n
        op.sem = key
        op.val = 16 * n
        op.signaled = True
        self.ops[queue].append(op)
        self.nops += 1
        return op

    def emit(self, final_wait_ops=()):
        nc = self.nc
        es = self.es
        for i, key in enumerate(self.dma_cnt):
            self.dma_sems[key] = es.enter_context(nc.semaphore("d%d" % i))
        for e in ENGS:
            self.eng_sems[e] = es.enter_context(nc.semaphore("e_" + e))
        for e in ENGS:
            t = 0
            for op in self.ops[e]:
                if op.is_dma:
                    op.sem = self.dma_sems[op.sem]
                elif op.signaled:
                    t += 1
                    op.sem = self.eng_sems[e]
                    op.val = t
        block = es.enter_context(nc.Block())
        engmap = {"pe": block.tensor, "act": block.scalar, "dve": block.vector,
                  "pool": block.gpsimd, "sp": block.sync}
        final_wait_ops = list(final_wait_ops)

        def make(ename):
            ops = self.ops[ename]

            def body(e):
                seen = {}
                for op in ops:
                    for d in op.deps:
                        s = d.sem
                        k = id(s)
                        if seen.get(k, 0) < d.val:
                            e.wait_ge(s, d.val)
                            seen[k] = d.val
                    inst = op.fn(e)
                    if op.signaled:
                        inst.then_inc(op.sem, 16 if op.is_dma else 1)
                if ename == "sp":
                    for d in final_wait_ops:
                        if seen.get(id(d.sem), 0) < d.val:
                            e.wait_ge(d.sem, d.val)
                            seen[id(d.sem)] = d.val
            return body

        for ename in ENGS:
            if self.ops[ename] or (ename == "sp" and final_wait_ops):
                engmap[ename](make(ename))


def _persist(k):
    if isinstance(k, tuple):
        return k[0] in ("HT", "QT", "KT", "V", "Z", "XBCT", "DT", "MIXT", "OUT", "CST")
    return k.startswith("c:") or k.startswith("w") or k.startswith("ps")


class Arena:
    def __init__(self, base, nwords):
        self.base = base
        self.nwords = nwords
        self.off = 0
        self.peak = 0

    def reset(self):
        self.off = 0

    def mark(self):
        return self.off

    def release(self, m):
        self.off = m

    def get(self, free_shape, dt):
        n = 1
        for s in free_shape:
            n *= s
        words = n if dt == F32 else (n + 1) // 2
        words = (words + 15) // 16 * 16
        assert self.off + words <= self.nwords, ("arena overflow", self.off, words, self.nwords)
        v = self.base[:, self.off:self.off + words]
        self.off += words
        self.peak = max(self.peak, self.off)
        if dt != F32:
            v = v.bitcast(dt)
        v = v[:, 0:n]
        if len(free_shape) == 2:
            v = v.rearrange("p (a b) -> p a b", a=free_shape[0])
        elif len(free_shape) == 3:
            v = v.rearrange("p (a b c) -> p a b c", a=free_shape[0], b=free_shape[1])
        return v


def mmgroup(items):
    def fn(e):
        inst = None
        for (o, l, r, s, t) in items:
            inst = e.matmul(o, l, r, start=s, stop=t)
        return inst
    return fn


def trgroup(items):
    def fn(e):
        inst = None
        for (o, i, ident) in items:
            inst = e.transpose(o, i, ident)
        return inst
    return fn


def build(depth=DEPTH, nseq=2, dbg=(), stop_after=None):
    nc = bass.Bass("TRN2", target_bir_lowering=False)

    def din(name, shape, dt=F32):
        return nc.dram_tensor(name, list(shape), dt, kind="ExternalInput").ap()

    def dscr(name, shape, dt):
        if name in dbg:
            return nc.dram_tensor(name, list(shape), dt, kind="ExternalOutput").ap()
        return nc.dram_tensor(name, list(shape), dt).ap()

    x_d = din("x", [nseq, L, D])
    wgu_d = [din("wgu1", [depth, NJ, 128, 4096]), din("wgu2", [depth, NJ, 128, 4096])]
    wd_d = [din("wd1", [depth, 16, 128, DFF]), din("wd2", [depth, 16, 128, DFF])]
    win_d = din("win", [depth, 128, WIN_TOT])
    wout_d = din("wout", [depth, 16, 128, 2048])
    pcol_d = din("pcol", [128, depth * NPC_L + 16])
    prow_d = din("prow", [1, depth * NPR_L])
    cmat_d = din("cmat", [128, 7 * 128])
    rope_d = din("rope", [128, 2, L])
    out_d = nc.dram_tensor("out", [nseq, L, D], F32, kind="ExternalOutput").ap()

    HT = [dscr("HT%d" % s, [16, 128, L], F32) for s in range(nseq)]
    QT = [dscr("QT%d" % s, [8, 128, L], BF16) for s in range(nseq)]
    KT = [dscr("KT%d" % s, [2, 128, L], BF16) for s in range(nseq)]
    VV = [dscr("V%d" % s, [16, 128, 256], BF16) for s in range(nseq)]
    ZZ = [dscr("Z%d" % s, [16, 128, 1024], BF16) for s in range(nseq)]
    XBCT = [dscr("XBCT%d" % s, [12, 128, L], F32) for s in range(nseq)]
    DTS = [dscr("DT%d" % s, [16, 128, 32], F32) for s in range(nseq)]
    MIXT = [dscr("MIXT%d" % s, [16, 128, L], BF16) for s in range(nseq)]
    CST = [dscr("CST%d" % s, [16, 32, 128], F32) for s in range(nseq)]

    es = ExitStack()
    P = Prog(nc, es)
    pcol = P.sb("pcol", [128, depth * NPC_L + 16], F32)
    cmat = P.sb("cmat", [128, 7 * 128], F32)
    cbf = P.sb("cbf", [128, 3 * 128], BF16)
    wgu_sb = [P.sb("wgu%d" % i, [128, 4096], BF16) for i in range(3)]
    wd_sb = [P.sb("wd%d" % i, [128, DFF], BF16) for i in range(2)]
    ARENA_W = 38 * 1024
    arena_t = P.sb("arena", [128, ARENA_W], F32)
    A = Arena(arena_t, ARENA_W)
    big = [P.ps("psb%d" % i, [128, 1024], F32) for i in range(4)]
    banks = [big[i // 2][:, (i % 2) * 512:(i % 2 + 1) * 512] for i in range(8)]
    BN = ["ps%d" % i for i in range(8)]

    ident_f = cmat[:, 0:128]
    triu_f = cmat[:, 128:256]
    tril_f = cmat[:, 256:384]
    neg_fb = cmat[:, 384:640]
    ones_f = cmat[:, 640:768]
    ident_b = cbf[:, 0:128]
    ones_b = cbf[:, 128:256]
    rot_b = cbf[:, 256:384]

    P.dma("sp", "c0", pcol[:], pcol_d, writes=["c:pcol"])
    P.dma("sp", "c1", cmat[:], cmat_d, writes=["c:cmat"])
    P.add("dve", lambda e: e.tensor_copy(cbf[:, 0:128], cmat[:, 0:128]), reads=["c:cmat"], writes=["c:cbf"])
    P.add("dve", lambda e: e.tensor_copy(cbf[:, 128:384], cmat[:, 640:896]), reads=["c:cmat"], writes=["c:cbf"])

    cnt = {"gu": 0, "wd": 0}
    final_ops = []

    def wslot_gu():
        s = cnt["gu"] % 3
        cnt["gu"] += 1
        return s

    def wslot_d():
        s = cnt["wd"] % 2
        cnt["wd"] += 1
        return s

    def bf_bank(i, shape):
        v = banks[i][:, :].bitcast(BF16)
        if len(shape) == 2:
            return v[:, 0:shape[0] * shape[1]].rearrange("p (a b) -> p a b", a=shape[0])
        return v[:, 0:shape[0]]

    def norm_to_uT(tag, s, t0, T, gc0, uT, nfeat_chunks=16, inv_n=1.0 / D, hin=None, hn=None, defer=False):
        nh = T // 512
        if hin is None:
            hin = [A.get([T], F32) for _ in range(3)]
        if hn is None:
            hn = [tag + "hin%d" % i for i in range(len(hin))]
        NS_ = len(hin)
        sq = [A.get([T], BF16) for _ in range(NS_)]
        rb = A.get([T], F32)
        for k in range(nfeat_chunks):
            sl = k % NS_
            P.dma("sp", hn[sl], hin[sl], HT[s][k, :, t0:t0 + T],
                  reads=[("HT", s, k)], writes=[hn[sl]])
            P.add("act", (lambda sl: lambda e: e.activation(sq[sl], hin[sl], AF.Square))(sl),
                  reads=[hn[sl]], writes=[tag + "sq%d" % sl])
            P.add("pe", mmgroup([(banks[4 + h][:, :], ones_b, sq[sl][:, h * 512:(h + 1) * 512], k == 0,
                                  k == nfeat_chunks - 1) for h in range(nh)]),
                  reads=[tag + "sq%d" % sl, "c:cbf"], writes=[BN[4 + h] for h in range(nh)])
            if defer:
                P.add("dve", (lambda sl, k: lambda e: e.tensor_scalar(uT[:, k, :], hin[sl], pcol[:, gc0 + k:gc0 + k + 1], 0.0,
                                                                      ALU.mult, ALU.add))(sl, k),
                      reads=[hn[sl], "c:pcol"], writes=[(tag + "uT", k)])
        for h in range(nh):
            P.add("act", (lambda h: lambda e: e.activation(rb[:, h * 512:(h + 1) * 512], banks[4 + h][:, :], AF.Ln,
                                                            bias=EPS, scale=inv_n))(h),
                  reads=[BN[4 + h]], writes=[tag + "rb"])
        P.add("act", lambda e: e.activation(rb, rb, AF.Exp, scale=-0.5), reads=[tag + "rb"], writes=[tag + "rb"])
        if defer:
            return rb
        for k in range(nfeat_chunks):
            sl = k % NS_
            P.dma("sp", hn[sl], hin[sl], HT[s][k, :, t0:t0 + T],
                  reads=[("HT", s, k)], writes=[hn[sl]])
            P.add("dve", (lambda sl, k: lambda e: e.scalar_tensor_tensor(uT[:, k, :], hin[sl], pcol[:, gc0 + k:gc0 + k + 1],
                                                                       rb, ALU.mult, ALU.mult))(sl, k),
                  reads=[hn[sl], tag + "rb", "c:pcol"], writes=[(tag + "uT", k)])

    def init_phase(s):
        P.barrier()
        A.reset()
        xin = [A.get([D], F32) for _ in range(2)]
        stg = [A.get([16, 512], F32) for _ in range(2)]
        for tg in range(4):
            st = stg[tg % 2]
            for c4 in range(4):
                c = tg * 4 + c4
                sl = c % 2
                P.dma("sp", "ixin%d" % sl, xin[sl], x_d[s, c * 128:(c + 1) * 128, :], writes=["ixin%d" % sl])
                for k4 in range(4):
                    bi = (c * 4 + k4) % 4
                    bv = banks[bi][:, :].rearrange("p (a b) -> p a b", a=4)
                    P.add("pe", trgroup([(bv[:, i, :], xin[sl][:, (k4 * 4 + i) * 128:(k4 * 4 + i + 1) * 128], ident_f)
                                         for i in range(4)]),
                          reads=["ixin%d" % sl, "c:cmat"], writes=[BN[bi]])
                    eng = "dve" if k4 % 2 == 0 else "act"
                    dst = st[:, k4 * 4:(k4 + 1) * 4, c4 * 128:(c4 + 1) * 128]
                    if eng == "dve":
                        P.add("dve", (lambda dst, bv: lambda e: e.tensor_copy(dst, bv))(dst, bv),
                              reads=[BN[bi]], writes=[("istg%d" % (tg % 2), c4, k4)])
                    else:
                        P.add("act", (lambda dst, bv: lambda e: e.copy(dst, bv))(dst, bv),
                              reads=[BN[bi]], writes=[("istg%d" % (tg % 2), c4, k4)])
            P.dma("sp", "istg%d" % (tg % 2), HT[s][:, :, tg * 512:(tg + 1) * 512].rearrange("k p t -> p k t"), st,
                  reads=[("istg%d" % (tg % 2), c4, k4) for c4 in range(4) for k4 in range(4)],
                  writes=[("HT", s, k) for k in range(16)])

    def ffn_phase(s, l, which):
        gc0 = l * NPC_L + (0 if which == 0 else 32)
        tag = "f"
        P.barrier()
        A.reset()
        uT = A.get([16, 1024], BF16)
        aT = A.get([NJ, 1024], BF16)
        hres = [A.get([1024], F32) for _ in range(3)]
        sq = [A.get([1024], BF16) for _ in range(3)]
        rb = A.get([1024], F32)
        sgbuf = A.get([4, 512], F32)
        sg = [sgbuf[:, 0, :], sgbuf[:, 1, :]]
        sgi = [sgbuf[:, 2, :], sgbuf[:, 3, :]]
        hinB = [sgbuf[:, 0:2, :].rearrange("p a b -> p (a b)"), sgbuf[:, 2:4, :].rearrange("p a b -> p (a b)")]
        hinB_n = [["fsg0", "fsg1"], ["fsgi0", "fsgi1"]]

        def norm_load(k, t0, hin_ap, names, key):
            P.dma("sp", key, hin_ap, HT[s][k, :, t0:t0 + 1024], reads=[("HT", s, k)], writes=names)

        def norm_sq_ut(k, hin_ap, names, sqi):
            P.add("act", (lambda: lambda e: e.activation(sq[sqi], hin_ap, AF.Square))(), reads=names,
                  writes=["fsq%d" % sqi])
            P.add("dve", (lambda: lambda e: e.tensor_scalar(uT[:, k, :], hin_ap, pcol[:, gc0 + k:gc0 + k + 1], 0.0,
                                                            ALU.mult, ALU.add))(),
                  reads=names + ["c:pcol"], writes=[(tag + "uT", k)])

        def norm_mm(k, sqi, b0):
            P.add("pe", mmgroup([(banks[b0 + h][:, :], ones_b, sq[sqi][:, h * 512:(h + 1) * 512], k == 0, k == 15)
                                 for h in range(2)]),
                  reads=["fsq%d" % sqi, "c:cbf"], writes=[BN[b0], BN[b0 + 1]])

        def norm_fin(b0):
            for h in range(2):
                P.add("act", (lambda h: lambda e: e.activation(rb[:, h * 512:(h + 1) * 512], banks[b0 + h][:, :], AF.Ln,
                                                                bias=EPS, scale=1.0 / D))(h),
                      reads=[BN[b0 + h]], writes=["frb"])
            P.add("act", lambda e: e.activation(rb, rb, AF.Exp, scale=-0.5), reads=["frb"], writes=["frb"])

        for k in range(16):
            sl = k % 3
            norm_load(k, 0, hres[sl], ["fhres%d" % sl], "fhres%d" % sl)
            norm_sq_ut(k, hres[sl], ["fhres%d" % sl], sl)
            norm_mm(k, sl, 4)
        norm_fin(4)
        for t in range(2):
            t0 = t * 1024
            for j in range(NJ):
                sl = wslot_gu()
                P.dma("pool", "w:gu%d" % sl, wgu_sb[sl][:, :], wgu_d[which][l, j], writes=["wgu%d" % sl])
                wv = wgu_sb[sl][:, :].rearrange("p (k c) -> p k c", k=16)
                for half in range(2):
                    i = j * 2 + half
                    gb, ub = (i % 2) * 2, (i % 2) * 2 + 1
                    rhs = lambda k: uT[:, k, half * 512:(half + 1) * 512]
                    P.add("pe", mmgroup([(banks[gb][:, :], wv[:, k, 0:128], rhs(k), k == 0, k == 15) for k in range(16)]),
                          reads=["wgu%d" % sl] + [(tag + "uT", k) for k in range(16)], writes=[BN[gb]])
                    P.add("pe", mmgroup([(banks[ub][:, :], wv[:, k, 128:256], rhs(k), k == 0, k == 15) for k in range(16)]),
                          reads=["wgu%d" % sl] + [(tag + "uT", k) for k in range(16)], writes=[BN[ub]])
                    rbh = rb[:, half * 512:(half + 1) * 512]
                    P.add("dve", (lambda i, gb, rbh: lambda e: e.tensor_tensor(sgi[i % 2], banks[gb][:, :], rbh, ALU.mult))(
                        i, gb, rbh), reads=[BN[gb], "frb"], writes=["fsgi%d" % (i % 2)])
                    P.add("act", (lambda i: lambda e: e.activation(sg[i % 2], sgi[i % 2], AF.Silu))(i),
                          reads=["fsgi%d" % (i % 2)], writes=["fsg%d" % (i % 2)])
                    P.add("dve", (lambda i, ub, rbh: lambda e: e.tensor_tensor(sgi[i % 2], banks[ub][:, :], rbh, ALU.mult))(
                        i, ub, rbh), reads=[BN[ub], "frb", "fsg%d" % (i % 2)], writes=["fsgi%d" % (i % 2)])
                    P.add("dve", (lambda i, j, half: lambda e: e.tensor_tensor(
                        aT[:, j, half * 512:(half + 1) * 512], sg[i % 2], sgi[i % 2], ALU.mult))(i, j, half),
                          reads=["fsg%d" % (i % 2), "fsgi%d" % (i % 2)], writes=[("faT", j, half)])
            pend = None
            for m in range(16):
                sl = wslot_d()
                P.dma("pool", "w:d%d" % sl, wd_sb[sl][:, :], wd_d[which][l, m], writes=["wd%d" % sl])
                wv = wd_sb[sl][:, :].rearrange("p (j c) -> p j c", j=NJ)
                hs = m % 3
                P.dma("sp", "fhres%d" % hs, hres[hs], HT[s][m, :, t0:t0 + 1024],
                      reads=[("HT", s, m)], writes=["fhres%d" % hs])
                for half in range(2):
                    bi = 4 + (m % 2) * 2 + half
                    P.add("pe", mmgroup([(banks[bi][:, :], wv[:, j, :], aT[:, j, half * 512:(half + 1) * 512], j == 0,
                                          j == NJ - 1) for j in range(NJ)]),
                          reads=["wd%d" % sl] + [("faT", j, half) for j in range(NJ)], writes=[BN[bi]])
                    P.add("dve", (lambda hs, half, bi: lambda e: e.scalar_tensor_tensor(
                        hres[hs][:, half * 512:(half + 1) * 512], banks[bi][:, :], 0.5,
                        hres[hs][:, half * 512:(half + 1) * 512], ALU.mult, ALU.add))(hs, half, bi),
                          reads=[BN[bi], "fhres%d" % hs], writes=["fhres%d" % hs])
                if t == 0 and pend is not None:
                    norm_mm(pend[0], pend[1], 0)
                    pend = None
                P.dma("sp", "fhst%d" % hs, HT[s][m, :, t0:t0 + 1024], hres[hs],
                      reads=["fhres%d" % hs], writes=[("HT", s, m)])
                if t == 0:
                    k = m
                    hb = k % 2
                    norm_load(k, 1024, hinB[hb], hinB_n[hb], "fhinB%d" % hb)
                    norm_sq_ut(k, hinB[hb], hinB_n[hb], k % 3)
                    pend = (k, k % 3)
            if t == 0:
                norm_mm(pend[0], pend[1], 0)
                norm_fin(0)

    def inproj_phase(s, l):
        pc = l * NPC_L
        P.barrier()
        A.reset()
        uT2 = [A.get([16, 1024], BF16) for _ in range(2)]
        nhin = [A.get([1024], F32) for _ in range(3)]
        nsq = [A.get([1024], BF16) for _ in range(3)]
        nrb = A.get([1024], F32)
        hn = ["inh%d" % i for i in range(3)]
        sqn = ["insq%d" % i for i in range(3)]
        cs_t = A.get([2, 1024], F32)
        sqb = [A.get([512], BF16) for _ in range(2)]
        lnv = [A.get([512], F32) for _ in range(2)]
        qn = [A.get([512], BF16) for _ in range(2)]
        t1 = [A.get([512], F32) for _ in range(2)]
        t2 = [A.get([512], F32) for _ in range(2)]
        qst = [A.get([1024], BF16) for _ in range(2)]
        xst = [A.get([1024], F32) for _ in range(2)]
        vst = A.get([8, 256], BF16)
        zst = [A.get([8, 256], BF16) for _ in range(2)]
        dst_ = A.get([8, 32], F32)
        sqb3 = sqb + [A.get([512], BF16)]
        gc0 = pc + 16

        def norm_steps(t0, uT, utag, b0):
            def mm(k):
                sl = k % 3
                P.add("pe", mmgroup([(banks[b0 + h][:, :], ones_b, nsq[sl][:, h * 512:(h + 1) * 512], k == 0, k == 15)
                                     for h in range(2)]), reads=[sqn[sl], "c:cbf"], writes=[BN[b0], BN[b0 + 1]])

            def p1(k):
                def f():
                    if k > 1:
                        mm(k - 2)
                    sl = k % 3
                    P.dma("sp", hn[sl], nhin[sl], HT[s][k, :, t0:t0 + 1024], reads=[("HT", s, k)], writes=[hn[sl]])
                    P.add("act", lambda e: e.activation(nsq[sl], nhin[sl], AF.Square), reads=[hn[sl]], writes=[sqn[sl]])
                return f

            def fin():
                mm(14)
                mm(15)
                for h in range(2):
                    P.add("act", (lambda h: lambda e: e.activation(nrb[:, h * 512:(h + 1) * 512], banks[b0 + h][:, :], AF.Ln,
                                                                    bias=EPS, scale=1.0 / D))(h),
                          reads=[BN[b0 + h]], writes=["inrb"])
                P.add("act", lambda e: e.activation(nrb, nrb, AF.Exp, scale=-0.5), reads=["inrb"], writes=["inrb"])

            def p2(k):
                def f():
                    sl = k % 3
                    P.dma("sp", hn[sl], nhin[sl], HT[s][k, :, t0:t0 + 1024], reads=[("HT", s, k)], writes=[hn[sl]])
                    P.add("dve", lambda e: e.scalar_tensor_tensor(uT[:, k, :], nhin[sl], pcol[:, gc0 + k:gc0 + k + 1],
                                                                  nrb, ALU.mult, ALU.mult),
                          reads=[hn[sl], "inrb", "c:pcol"], writes=[(utag, k)])
                return f
            return [p1(k) for k in range(16)] + [fin] + [p2(k) for k in range(16)]

        for st in norm_steps(0, uT2[0], "iuT0", 4):
            st()
        pending = []

        def pump():
            if pending:
                pending.pop(0)()

        for t in range(2):
            t0 = t * 1024
            uT = uT2[t]
            P.dma("sp", "irope", cs_t, rope_d[:, :, t0:t0 + 1024], writes=["irope"])
            uTr = [("iuT%d" % t, k) for k in range(16)]

            def load_w(name):
                off, ncols = WIN_OFF[name]
                sl = wslot_gu()
                P.dma("pool", "w:gu%d" % sl, wgu_sb[sl][:, 0:16 * ncols], win_d[l, :, off:off + 16 * ncols],
                      writes=["wgu%d" % sl])
                return sl, wgu_sb[sl][:, 0:16 * ncols].rearrange("p (k c) -> p k c", k=16)

            it = 0
            winfo = {}

            def stA(n):
                b_, half = divmod(n, 2)
                if half == 0:
                    name = ("q%d" % b_) if b_ < 8 else ("k%d" % (b_ - 8))
                    winfo[b_] = load_w(name)
                sl, wv = winfo[b_]
                ba = n % 3
                P.add("pe", mmgroup([(banks[ba][:, :], wv[:, k, :], uT[:, k, half * 512:(half + 1) * 512], k == 0, k == 15)
                                     for k in range(16)]), reads=["wgu%d" % sl] + uTr, writes=[BN[ba]])
                P.add("act", (lambda ba: lambda e: e.activation(sqb3[ba], banks[ba][:, :], AF.Square))(ba),
                      reads=[BN[ba]], writes=["isq%d" % ba])

            def stB(n):
                b_, half = divmod(n, 2)
                ba = n % 3
                r = n % 2
                bb = 3 + r
                gcol = pcol[:, pc + 56:pc + 57] if b_ < 8 else pcol[:, pc + 57:pc + 58]
                P.add("pe", mmgroup([(banks[bb][:, :], ones_b, sqb3[ba], True, True)]), reads=["isq%d" % ba, "c:cbf"],
                      writes=[BN[bb]])
                P.add("act", (lambda r, bb: lambda e: e.activation(lnv[r], banks[bb][:, :], AF.Ln, bias=EPS,
                                                                   scale=1.0 / 128))(r, bb),
                      reads=[BN[bb]], writes=["iln%d" % r])
                P.add("act", (lambda r: lambda e: e.activation(lnv[r], lnv[r], AF.Exp, scale=-0.5))(r),
                      reads=["iln%d" % r], writes=["iln%d" % r])
                P.add("dve", (lambda r, ba, gcol: lambda e: e.scalar_tensor_tensor(
                    qn[r], banks[ba][:, :], gcol, lnv[r], ALU.mult, ALU.mult))(r, ba, gcol),
                      reads=[BN[ba], "iln%d" % r, "c:pcol"], writes=["iqn%d" % r])

            def stC(n):
                b_, half = divmod(n, 2)
                r = n % 2
                bc = 5 + r
                qs = qst[b_ % 2]
                P.add("pe", mmgroup([(banks[bc][:, :], rot_b, qn[r], True, True)]), reads=["iqn%d" % r, "c:cbf"],
                      writes=[BN[bc]])
                P.add("dve", (lambda r, half: lambda e: e.tensor_tensor(
                    t1[r], qn[r], cs_t[:, 0, half * 512:(half + 1) * 512], ALU.mult))(r, half),
                      reads=["iqn%d" % r, "irope"], writes=["it1%d" % r])
                P.add("dve", (lambda r, half, bc: lambda e: e.tensor_tensor(
                    t2[r], banks[bc][:, :], cs_t[:, 1, half * 512:(half + 1) * 512], ALU.mult))(r, half, bc),
                      reads=[BN[bc], "irope"], writes=["it2%d" % r])
                P.add("dve", (lambda r, half, qs: lambda e: e.tensor_tensor(
                    qs[:, half * 512:(half + 1) * 512], t1[r], t2[r], ALU.add))(r, half, qs),
                      reads=["it1%d" % r, "it2%d" % r], writes=["iqst%d" % (b_ % 2)])
                if half == 1:
                    dst = QT[s][b_, :, t0:t0 + 1024] if b_ < 8 else KT[s][b_ - 8, :, t0:t0 + 1024]
                    P.dma("sp", "iqst%d" % (b_ % 2), dst, qs, reads=["iqst%d" % (b_ % 2)],
                          writes=[("QT", s, b_) if b_ < 8 else ("KT", s, b_ - 8)])

            NQ = 20
            for n in range(NQ + 2):
                if n < NQ:
                    stA(n)
                if 0 <= n - 1 < NQ:
                    stB(n - 1)
                if 0 <= n - 2 < NQ:
                    stC(n - 2)
            for b in range(12):
                sl, wv = load_w("x%d" % b)
                xs_ = xst[b % 2]
                for half in range(2):
                    r = it % 2
                    it += 1
                    ba = 6 + r
                    P.add("pe", mmgroup([(banks[ba][:, :], wv[:, k, :], uT[:, k, half * 512:(half + 1) * 512], k == 0, k == 15)
                                         for k in range(16)]), reads=["wgu%d" % sl] + uTr, writes=[BN[ba]])
                    if half == 0:
                        P.add("act", (lambda xs_, ba: lambda e: e.copy(xs_[:, 0:512], banks[ba][:, :]))(xs_, ba),
                              reads=[BN[ba]], writes=["ixst%d" % (b % 2)])
                    else:
                        P.add("dve", (lambda xs_, ba: lambda e: e.tensor_copy(xs_[:, 512:1024], banks[ba][:, :]))(xs_, ba),
                              reads=[BN[ba]], writes=["ixst%d" % (b % 2)])
                P.dma("sp", "ixst%d" % (b % 2), XBCT[s][b, :, t0:t0 + 1024], xs_, reads=["ixst%d" % (b % 2)],
                      writes=[("XBCT", s, b)])
            if t == 0:
                pending.extend(norm_steps(1024, uT2[1], "iuT1", 6))
            sl, wv = load_w("v")
            for c in range(8):
                ba = c % 2
                P.add("pe", mmgroup([(banks[ba][:, 0:256], uT[:, k, c * 128:(c + 1) * 128], wv[:, k, :], k == 0, k == 15)
                                     for k in range(16)]), reads=["wgu%d" % sl] + uTr, writes=[BN[ba]])
                P.add("dve", (lambda c, ba: lambda e: e.tensor_copy(vst[:, c, :], banks[ba][:, 0:256]))(c, ba),
                      reads=[BN[ba]], writes=[("ivst", c)])
                pump()
            P.dma("sp", "ivst", VV[s][t * 8:(t + 1) * 8].rearrange("c p f -> p c f"), vst,
                  reads=[("ivst", c) for c in range(8)], writes=[("V", s)])
            sl, wv = load_w("dt")
            for c in range(8):
                ba = 2 + c % 2
                P.add("pe", mmgroup([(banks[ba][:, 0:32], uT[:, k, c * 128:(c + 1) * 128], wv[:, k, :], k == 0, k == 15)
                                     for k in range(16)]), reads=["wgu%d" % sl] + uTr, writes=[BN[ba]])
                P.add("dve", (lambda c, ba: lambda e: e.tensor_copy(dst_[:, c, :], banks[ba][:, 0:32]))(c, ba),
                      reads=[BN[ba]], writes=[("idst", c)])
                pump()
            P.dma("sp", "idst", DTS[s][t * 8:(t + 1) * 8].rearrange("c p f -> p c f"), dst_,
                  reads=[("idst", c) for c in range(8)], writes=[("DT", s)])
            for zb in range(4):
                sl, wv = load_w("z%d" % zb)
                zs_ = zst[zb % 2]
                for c in range(8):
                    ba = 4 + c % 2
                    P.add("pe", mmgroup([(banks[ba][:, 0:256], uT[:, k, c * 128:(c + 1) * 128], wv[:, k, :], k == 0, k == 15)
                                         for k in range(16)]), reads=["wgu%d" % sl] + uTr, writes=[BN[ba]])
                    P.add("act", (lambda c, ba, zs_: lambda e: e.activation(zs_[:, c, :], banks[ba][:, 0:256], AF.Silu))(
                        c, ba, zs_), reads=[BN[ba]], writes=[("izst%d" % (zb % 2), c)])
                    pump()
                P.dma("sp", "izst%d" % (zb % 2),
                      ZZ[s][t * 8:(t + 1) * 8, :, zb * 256:(zb + 1) * 256].rearrange("c p f -> p c f"), zs_,
                      reads=[("izst%d" % (zb % 2), c) for c in range(8)], writes=[("Z", s, zb)])
            while pending:
                pump()

    def ssd_phase(s, l):
        pc = l * NPC_L
        pr = l * NPR_L
        P.barrier()
        A.reset()
        rowp = A.get([1104], F32)
        P.dma("sp", "srowp", rowp, prow_d[:, pr:pr + 1104].partition_broadcast(128), writes=["srowp"])
        gain = rowp[:, 0:1024]
        dtb = rowp[:, 1024:1056]
        alog = rowp[:, 1056:1088]
        dsk = rowp[:, 1088:1104]
        xs_tok = A.get([16, 1024], BF16)
        btok = A.get([16, 256], BF16)
        bcT = A.get([4, L], BF16)
        dtr = A.get([16, 32], F32)
        dt_ = A.get([16, 32], F32)
        dtA = A.get([16, 32], F32)
        cs = A.get([16, 32], F32)
        negcs = A.get([16, 32], F32)
        atot = A.get([16, 32], F32)
        ecs = A.get([16, 32], F32)
        dtd = A.get([16, 32], F32)
        eat = A.get([16, 32], F32)
        negA = A.get([32], F32)
        mk = A.mark()
        mx = A.get([16, 32], F32)
        nab = A.get([16, 32], F32)
        csTf = A.get([L], F32)
        csTb = A.get([L], F32)
        xin = [A.get([L + 4], F32) for _ in range(2)]
        acc = [A.get([L], F32) for _ in range(2)]
        xo = [A.get([L], BF16) for _ in range(2)]
        for i in range(2):
            P.add("dve", (lambda i: lambda e: e.memset(xin[i][:, 0:2], 0.0))(i), writes=["sxin%d" % i])
            P.add("dve", (lambda i: lambda e: e.memset(xin[i][:, L + 2:L + 4], 0.0))(i), writes=["sxin%d" % i])
        def conv_s1(b):
            sl = b % 2
            P.dma("sp", "sxin%d" % sl, xin[sl][:, 2:L + 2], XBCT[s][b], reads=[("XBCT", s, b)], writes=["sxin%d" % sl])
            wc = pc + 58 + b * 5
            P.add("act", (lambda sl, wc: lambda e: e.activation(acc[sl], xin[sl][:, 0:L], AF.Copy,
                                                                 scale=pcol[:, wc:wc + 1]))(sl, wc),
                  reads=["sxin%d" % sl, "c:pcol"], writes=["sacc%d" % sl])

        def conv_s2(b):
            sl = b % 2
            wc = pc + 58 + b * 5
            for k in range(1, 5):
                P.add("dve", (lambda sl, wc, k: lambda e: e.scalar_tensor_tensor(
                    acc[sl], xin[sl][:, k:k + L], pcol[:, wc + k:wc + k + 1], acc[sl], ALU.mult, ALU.add))(sl, wc, k),
                      reads=["sxin%d" % sl, "sacc%d" % sl], writes=["sacc%d" % sl])

        def conv_s3(b):
            sl = b % 2
            if b < 10:
                dst = xo[sl]
                dname = "sxo%d" % sl
            else:
                dst = bcT[:, b - 8, :]
                dname = ("sbcT", b - 8)
            P.add("act", (lambda sl, dst, b: lambda e: e.activation(dst, acc[sl], AF.Silu,
                                                                    bias=pcol[:, pc + 118 + b:pc + 119 + b]))(sl, dst, b),
                  reads=["sacc%d" % sl, "c:pcol"], writes=[dname])
            if b in (8, 9):
                P.add("dve", (lambda sl, b: lambda e: e.tensor_copy(bcT[:, b - 8, :], xo[sl]))(sl, b),
                      reads=[dname], writes=[("sbcT", b - 8)])
            if b < 10:
                for half in range(2):
                    bi = half
                    bv = bf_bank(bi, [8, 128])
                    P.add("pe", trgroup([(bv[:, i, :], xo[sl][:, (half * 8 + i) * 128:(half * 8 + i + 1) * 128], ident_b)
                                         for i in range(8)]), reads=[dname, "c:cbf"], writes=[BN[bi]])
                    if b < 8:
                        dst2 = xs_tok[:, half * 8:(half + 1) * 8, b * 128:(b + 1) * 128]
                        rname = ("sxs", b, half)
                    else:
                        dst2 = btok[:, half * 8:(half + 1) * 8, (b - 8) * 128:(b - 7) * 128]
                        rname = ("sbtok", b, half)
                    if half == 0:
                        P.add("act", (lambda dst2, bv: lambda e: e.copy(dst2, bv))(dst2, bv), reads=[BN[bi]], writes=[rname])
                    else:
                        P.add("dve", (lambda dst2, bv: lambda e: e.tensor_copy(dst2, bv))(dst2, bv), reads=[BN[bi]],
                              writes=[rname])

        conv_s1(0)
        for b in range(12):
            if b + 1 < 12:
                conv_s1(b + 1)
            conv_s2(b)
            conv_s3(b)
        xs_r = [("sxs", b, h) for b in range(8) for h in range(2)]
        btok_r = [("sbtok", b, h) for b in (8, 9) for h in range(2)]
        P.dma("sp", "sdtr", dtr, DTS[s].rearrange("c p f -> p c f"), reads=[("DT", s)], writes=["sdtr"])
        bc3 = lambda ap: ap.unsqueeze(1).broadcast_to([128, 16, 32])
        P.add("dve", lambda e: e.tensor_tensor(dtr, dtr, bc3(dtb), ALU.add), reads=["sdtr", "srowp"], writes=["sdtr"])
        P.add("dve", lambda e: e.tensor_scalar_max(mx, dtr, 0.0), reads=["sdtr"], writes=["smx"])
        P.add("dve", lambda e: e.scalar_tensor_tensor(nab, mx, -2.0, dtr, ALU.mult, ALU.add), reads=["smx", "sdtr"],
              writes=["snab"])
        P.add("act", lambda e: e.activation(nab, nab, AF.Exp), reads=["snab"], writes=["snab"])
        P.add("act", lambda e: e.activation(nab, nab, AF.Ln, bias=1.0), reads=["snab"], writes=["snab"])
        P.add("dve", lambda e: e.tensor_tensor(dt_, mx, nab, ALU.add), reads=["smx", "snab"], writes=["sdt"])
        P.add("act", lambda e: e.activation(negA, alog, AF.Exp), reads=["srowp"], writes=["snegA"])
        P.add("dve", lambda e: e.tensor_scalar(negA, negA, -1.0, 0.0, ALU.mult, ALU.add), reads=["snegA"], writes=["snegA"])
        P.add("dve", lambda e: e.tensor_tensor(dtA, dt_, bc3(negA), ALU.mult), reads=["sdt", "snegA"], writes=["sdtA"])
        csb = banks[2][:, :].rearrange("p (c f) -> p c f", c=16)
        items = []
        for c in range(16):
            items.append((csb[:, c, 0:16], triu_f, dtA[:, c, 0:16], True, True))
            items.append((csb[:, c, 16:32], tril_f, dtA[:, c, 16:32], True, True))
        P.add("pe", mmgroup(items), reads=["sdtA", "c:cmat"], writes=[BN[2]])
        P.add("pe", mmgroup([(banks[3][:, :], ones_f, dtA.rearrange("p c f -> p (c f)"), True, True)]),
              reads=["sdtA", "c:cmat"], writes=[BN[3]])
        P.add("dve", lambda e: e.tensor_copy(cs, csb), reads=[BN[2]], writes=["scs"])
        P.add("dve", lambda e: e.tensor_scalar(negcs, cs, -1.0, 0.0, ALU.mult, ALU.add), reads=["scs"], writes=["snegcs"])
        P.add("dve", lambda e: e.tensor_copy(atot.rearrange("p c f -> p (c f)"), banks[3][:, :]), reads=[BN[3]],
              writes=["satot"])
        P.add("act", lambda e: e.activation(ecs, cs, AF.Exp), reads=["scs"], writes=["secs"])
        P.add("dve", lambda e: e.tensor_tensor(dtd, atot, cs, ALU.subtract), reads=["satot", "scs"], writes=["sdtd"])
        P.add("act", lambda e: e.activation(dtd, dtd, AF.Exp), reads=["sdtd"], writes=["sdtd"])
        P.add("dve", lambda e: e.tensor_tensor(dtd, dtd, dt_, ALU.mult), reads=["sdtd", "sdt"], writes=["sdtd"])
        P.add("act", lambda e: e.activation(eat, atot, AF.Exp), reads=["satot"], writes=["seat"])
        for c in range(16):
            P.add("pe", mmgroup([
                (banks[c // 4][0:32, (c % 4) * 128:(c % 4 + 1) * 128], dtA[:, c, :], triu_f, True, True),
                (banks[4 + c // 4][0:32, (c % 4) * 128:(c % 4 + 1) * 128], dtA[:, c, :], tril_f, True, True)]),
                  reads=["sdtA", "c:cmat"], writes=[BN[c // 4], BN[4 + c // 4]])
        for q4 in range(4):
            P.add("act", (lambda q4: lambda e: e.copy(csTf[0:32, q4 * 512:(q4 + 1) * 512], banks[q4][0:32, :]))(q4),
                  reads=[BN[q4]], writes=[("scsTf", q4)])
            P.add("dve", (lambda q4: lambda e: e.tensor_copy(csTb[0:32, q4 * 512:(q4 + 1) * 512], banks[4 + q4][0:32, :]))(q4),
                  reads=[BN[4 + q4]], writes=[("scsTb", q4)])
        P.dma("sp", "scsTf", CST[s][:, 0:16, :].rearrange("c h l -> h c l"),
              csTf[0:16, :].rearrange("h (c l) -> h c l", c=16),
              reads=[("scsTf", q4) for q4 in range(4)], writes=[("CST", s, 0)])
        P.dma("sp", "scsTb", CST[s][:, 16:32, :].rearrange("c h l -> h c l"),
              csTb[16:32, :].rearrange("h (c l) -> h c l", c=16),
              reads=[("scsTb", q4) for q4 in range(4)], writes=[("CST", s, 1)])
        P.barrier()
        A.release(mk)
        sinb = A.get([16, 512], BF16)
        zsl = [A.get([512], BF16) for _ in range(2)]
        sTs = [A.get([4, 128], BF16) for _ in range(2)]
        Sb = A.get([512], F32)
        Sf = A.get([512], F32)
        Sfb = A.get([512], BF16)
        Xd = [A.get([512], BF16) for _ in range(3)]
        Xf = [A.get([512], BF16) for _ in range(3)]
        Xb = [A.get([512], BF16) for _ in range(3)]
        crb = [A.get([2, 8, 128], F32) for _ in range(2)]
        Gt = [A.get([2, 8, 128], BF16) for _ in range(2)]
        cbtm = [A.get([2, 128], F32) for _ in range(2)]
        ta = [A.get([512], F32) for _ in range(2)]
        tb = [A.get([512], F32) for _ in range(2)]
        yo = [A.get([512], BF16) for _ in range(2)]
        td = [A.get([512], BF16) for _ in range(3)]
        ssq = A.get([8], F32)
        mask01 = cmat[:, 128:384].rearrange("p (a b) -> p a b", a=2)
        h3 = lambda ap: ap.rearrange("p (h q) -> p h q", h=8)
        hb3 = lambda ap: ap.unsqueeze(2).broadcast_to([128, 8, 64])
        cst_r = [("CST", s, 0), ("CST", s, 1)]
        for g in range(2):
            P.add("dve", lambda e: e.memset(Sb, 0.0), writes=["sSb"])
            P.add("dve", lambda e: e.memset(Sf, 0.0), writes=["sSf"])
            P.add("dve", lambda e: e.memset(Sfb, 0.0), writes=["sSfb"])
            for c in range(15, -1, -1):
                r = c % 2
                xsl = xs_tok[:, c, g * 512:(g + 1) * 512]
                hbo = 16 + g * 8
                P.add("pool", (lambda r, xsl, c, hbo: lambda e: e.tensor_tensor(
                    h3(Xd[r]), h3(xsl), hb3(dtd[:, c, hbo:hbo + 8]), ALU.mult))(r, xsl, c, hbo),
                      reads=xs_r + ["sdtd"], writes=["sXd%d" % r])
                sbk = 6 if c % 2 else 3
                P.add("pe", mmgroup([(banks[sbk][:, :], btok[:, c, g * 128:(g + 1) * 128], Xd[r], True, True)]),
                      reads=btok_r + ["sXd%d" % r], writes=[BN[sbk]])
                P.add("act", (lambda c: lambda e: e.copy(sinb[:, c, :], Sb))(c), reads=["sSb"], writes=[("ssinb", c)])
                P.add("dve", (lambda c, hbo: lambda e: e.tensor_tensor(h3(Sb), h3(Sb), hb3(eat[:, c, hbo:hbo + 8]),
                                                                        ALU.mult))(c, hbo),
                      reads=["sSb", "seat"], writes=["sSb"])
                P.add("dve", (lambda sbk: lambda e: e.tensor_tensor(Sb, Sb, banks[sbk][:, :], ALU.add))(sbk),
                      reads=["sSb", BN[sbk]], writes=["sSb"])

            def load_cr(c):
                r = c % 2
                hfo = g * 8
                hbo = 16 + g * 8
                P.dma("sp", "scrbf%d" % r, crb[r][:, 0, :, :].rearrange("p h l -> p (h l)"),
                      CST[s][c:c + 1, hfo:hfo + 8, :].rearrange("c h l -> c (h l)").partition_broadcast(128),
                      reads=cst_r, writes=[("scrb%d" % r, 0)] + [("scrbL%d" % r, 0, h) for h in range(8)])
                P.dma("sp", "scrbb%d" % r, crb[r][:, 1, :, :].rearrange("p h l -> p (h l)"),
                      CST[s][c:c + 1, hbo:hbo + 8, :].rearrange("c h l -> c (h l)").partition_broadcast(128),
                      reads=cst_r, writes=[("scrb%d" % r, 1)] + [("scrbL%d" % r, 1, h) for h in range(8)])

            def load_z(c):
                r = c % 2
                P.dma("sp", "szsl%d" % r, zsl[r], ZZ[s][c, :, g * 512:(g + 1) * 512],
                      reads=[("Z", s, zb) for zb in range(4)], writes=["szsl%d" % r])

            load_cr(0)
            load_z(0)
            load_cr(1)
            load_z(1)

            def front_a(c):
                r = c % 2
                x3 = c % 3
                xsl = xs_tok[:, c, g * 512:(g + 1) * 512]
                hfo = g * 8
                hbo = 16 + g * 8
                csl = slice(c * 128, (c + 1) * 128)
                P.add("pe", mmgroup([(banks[0][:, 0:128], bcT[:, g, csl], bcT[:, 2 + g, csl], True, True)]),
                      reads=[("sbcT", g), ("sbcT", 2 + g)], writes=[BN[0]])
                P.add("pool", (lambda x3, xsl, c, hfo: lambda e: e.tensor_tensor(
                    h3(Xf[x3]), h3(xsl), hb3(dt_[:, c, hfo:hfo + 8]), ALU.mult))(x3, xsl, c, hfo),
                      reads=xs_r + ["sdt"], writes=["sXf%d" % x3])
                P.add("pool", (lambda x3, xsl, c, hbo: lambda e: e.tensor_tensor(
                    h3(Xb[x3]), h3(xsl), hb3(dt_[:, c, hbo:hbo + 8]), ALU.mult))(x3, xsl, c, hbo),
                      reads=xs_r + ["sdt"], writes=["sXb%d" % x3])
                P.add("pool", (lambda x3, xsl, c, hfo: lambda e: e.tensor_tensor(
                    h3(Xd[x3]), h3(xsl), hb3(dtd[:, c, hfo:hfo + 8]), ALU.mult))(x3, xsl, c, hfo),
                      reads=xs_r + ["sdtd"], writes=["sXd%d" % x3])
                P.add("pool", (lambda x3, xsl, hfo: lambda e: e.tensor_tensor(
                    h3(td[x3]), h3(xsl), hb3(dsk[:, hfo:hfo + 8]), ALU.mult))(x3, xsl, hfo),
                      reads=xs_r + ["srowp"], writes=["std%d" % x3])
                P.add("dve", (lambda r: lambda e: e.tensor_tensor(
                    cbtm[r], mask01, banks[0][:, 0:128].unsqueeze(1).broadcast_to([128, 2, 128]), ALU.mult))(r),
                      reads=[BN[0], "c:cmat"], writes=["scbtm%d" % r])
                for d in range(2):
                    for h in range(8):
                        hd = (hfo if d == 0 else hbo) + h
                        P.add("act", (lambda r, d, h, c, hd: lambda e: e.activation(
                            crb[r][:, d, h, :], crb[r][:, d, h, :], AF.Exp, bias=negcs[:, c, hd:hd + 1]))(r, d, h, c, hd),
                              reads=[("scrb%d" % r, d), "snegcs"], writes=[("scrbL%d" % r, d, h)])

            def front_b(c):
                r = c % 2
                for d in range(2):
                    P.add("dve", (lambda r, d: lambda e: e.scalar_tensor_tensor(
                        Gt[r][:, d, :, :], crb[r][:, d, :, :], 1.0,
                        cbtm[r][:, d, :].unsqueeze(1).broadcast_to([128, 8, 128]), ALU.min, ALU.mult))(r, d),
                          reads=[("scrbL%d" % r, d, h) for h in range(8)] + ["scbtm%d" % r],
                          writes=[("sGt%d" % r, d)])

            def back_a(c):
                r = c % 2
                x3 = c % 3
                hfo = g * 8
                hbo = 16 + g * 8
                csl = slice(c * 128, (c + 1) * 128)
                yb = 1 + r
                items = []
                for h in range(8):
                    items.append((banks[yb][:, h * 64:(h + 1) * 64], Gt[r][:, 0, h, :], Xf[x3][:, h * 64:(h + 1) * 64], True, False))
                    items.append((banks[yb][:, h * 64:(h + 1) * 64], Gt[r][:, 1, h, :], Xb[x3][:, h * 64:(h + 1) * 64], False, True))
                P.add("pe", mmgroup(items), reads=[("sGt%d" % r, 0), ("sGt%d" % r, 1), "sXf%d" % x3, "sXb%d" % x3],
                      writes=[BN[yb]])
                P.add("pe", mmgroup([(banks[4][:, :], bcT[:, 2 + g, csl], Sfb, True, True)]),
                      reads=[("sbcT", 2 + g), "sSfb"], writes=[BN[4]])
                P.add("pe", mmgroup([(banks[5][:, :], bcT[:, 2 + g, csl], sinb[:, c, :], True, True)]),
                      reads=[("sbcT", 2 + g), ("ssinb", c)], writes=[BN[5]])
                P.add("dve", (lambda r, c, hfo: lambda e: e.tensor_tensor(
                    h3(ta[r]), h3(banks[4][:, :]), hb3(ecs[:, c, hfo:hfo + 8]), ALU.mult))(r, c, hfo),
                      reads=[BN[4], "secs"], writes=["sta%d" % r])
                P.add("dve", (lambda r, c, hbo: lambda e: e.tensor_tensor(
                    h3(tb[r]), h3(banks[5][:, :]), hb3(ecs[:, c, hbo:hbo + 8]), ALU.mult))(r, c, hbo),
                      reads=[BN[5], "secs"], writes=["stb%d" % r])
                P.add("dve", (lambda r, yb: lambda e: e.tensor_tensor(ta[r], ta[r], banks[yb][:, :], ALU.add))(r, yb),
                      reads=["sta%d" % r, BN[yb]], writes=["sta%d" % r])
                P.add("dve", (lambda r: lambda e: e.tensor_tensor(ta[r], ta[r], tb[r], ALU.add))(r),
                      reads=["sta%d" % r, "stb%d" % r], writes=["sta%d" % r])
                P.add("dve", (lambda r, x3: lambda e: e.tensor_tensor(ta[r], ta[r], td[x3], ALU.add))(r, x3),
                      reads=["sta%d" % r, "std%d" % x3], writes=["sta%d" % r])
                P.add("dve", (lambda r: lambda e: e.tensor_tensor(ta[r], ta[r], zsl[r], ALU.mult))(r),
                      reads=["sta%d" % r, "szsl%d" % r], writes=["sta%d" % r])
                P.add("dve", (lambda r: lambda e: e.memset(ssq[:, r * 4:r * 4 + 1], 0.0))(r), writes=["sssq%d" % r])
                P.add("act", (lambda r: lambda e: e.activation(yo[r], ta[r], AF.Square, accum_out=ssq[:, r * 4:r * 4 + 1]))(r),
                      reads=["sta%d" % r, "sssq%d" % r], writes=["sssq%d" % r, "syo%d" % r])
                P.add("act", (lambda r: lambda e: e.activation(ssq[:, r * 4 + 1:r * 4 + 2], ssq[:, r * 4:r * 4 + 1], AF.Ln,
                                                               bias=EPS, scale=1.0 / 512))(r),
                      reads=["sssq%d" % r], writes=["sssq%d" % r])
                P.add("act", (lambda r: lambda e: e.activation(ssq[:, r * 4 + 2:r * 4 + 3], ssq[:, r * 4 + 1:r * 4 + 2], AF.Exp,
                                                               scale=-0.5))(r),
                      reads=["sssq%d" % r], writes=["sssq%d" % r])

            def back_b(c):
                r = c % 2
                x3 = c % 3
                hfo = g * 8
                P.add("dve", (lambda r, g: lambda e: e.scalar_tensor_tensor(
                    yo[r], ta[r], ssq[:, r * 4 + 2:r * 4 + 3], gain[:, g * 512:(g + 1) * 512], ALU.mult, ALU.mult))(r, g),
                      reads=["sta%d" % r, "sssq%d" % r, "srowp"], writes=["syo%d" % r])
                bv = bf_bank(7, [4, 128])
                P.add("pe", trgroup([(bv[:, i, :], yo[r][:, i * 128:(i + 1) * 128], ident_b) for i in range(4)]),
                      reads=["syo%d" % r, "c:cbf"], writes=[BN[7]])
                P.add("act", (lambda r, bv: lambda e: e.copy(sTs[r], bv))(r, bv),
                      reads=[BN[7]], writes=["ssT%d" % r])
                P.dma("sp", "ssT%d" % r, MIXT[s][8 + g * 4:12 + g * 4, :, c * 128:(c + 1) * 128].rearrange("k p t -> p k t"),
                      sTs[r], reads=["ssT%d" % r], writes=[("MIXT", s, "s", g, c)])
                P.add("pe", mmgroup([(banks[6][:, :], btok[:, c, g * 128:(g + 1) * 128], Xd[x3], True, True)]),
                      reads=btok_r + ["sXd%d" % x3], writes=[BN[6]])
                P.add("dve", (lambda c, hfo: lambda e: e.tensor_tensor(h3(Sf), h3(Sf), hb3(eat[:, c, hfo:hfo + 8]),
                                                                        ALU.mult))(c, hfo),
                      reads=["sSf", "seat"], writes=["sSf"])
                P.add("dve", lambda e: e.tensor_tensor(Sf, Sf, banks[6][:, :], ALU.add), reads=["sSf", BN[6]],
                      writes=["sSf"])
                P.add("act", lambda e: e.copy(Sfb, Sf), reads=["sSf"], writes=["sSfb"])

            front_a(0)
            front_b(0)
            load_cr(2)
            front_a(1)
            front_b(1)
            for c in range(16):
                if c + 3 < 16:
                    load_cr(c + 3)
                if c + 2 < 16:
                    front_a(c + 2)
                back_a(c)
                if c + 2 < 16:
                    front_b(c + 2)
                back_b(c)
                if c + 2 < 16:
                    load_z(c + 2)

    def attn_phase(s, l):
        pc = l * NPC_L
        pr = l * NPR_L
        P.barrier()
        A.reset()
        kT = A.get([2, L], BF16)
        vv = A.get([16, 256], BF16)
        qT = A.get([8, L], BF16)
        oT = A.get([8, L], F32)
        qk = A.get([256], F32)
        negB = A.get([4], F32)
        pt = [A.get([1024], BF16) for _ in range(3)]
        rden = [A.get([512], F32) for _ in range(2)]
        P.dma("sp", "akT", kT, KT[s].rearrange("k p t -> p k t"), reads=[("KT", s, 0), ("KT", s, 1)], writes=["akT"])
        P.dma("sp", "avv", vv, VV[s].rearrange("c p f -> p c f"), reads=[("V", s)], writes=["avv"])
        P.dma("sp", "aqT", qT, QT[s].rearrange("k p t -> p k t"), reads=[("QT", s, b) for b in range(8)], writes=["aqT"])
        P.dma("sp", "aqk", qk, prow_d[:, pr + 1104:pr + 1360].partition_broadcast(128), writes=["aqk"])
        P.add("dve", lambda e: e.tensor_reduce(negB[:, 0:1], qk[:, 0:128], AX.X, ALU.max, apply_absolute_value=True),
              reads=["aqk"], writes=["anegB"])
        P.add("dve", lambda e: e.tensor_reduce(negB[:, 1:2], qk[:, 128:256], AX.X, ALU.max, apply_absolute_value=True),
              reads=["aqk", "anegB"], writes=["anegB"])
        scale = 128.0 ** -0.5
        P.add("dve", lambda e: e.scalar_tensor_tensor(negB[:, 2:3], negB[:, 0:1], -scale * 128.0, negB[:, 1:2],
                                                      ALU.mult, ALU.mult), reads=["anegB"], writes=["anegB"])
        items = [(kv, qg, kv * 4 + h4) for kv in range(2) for qg in range(4) for h4 in range(4)]
        NS = len(items) * 8
        LOOK = 1

        def emit_S(n):
            it, s2 = divmod(n, 8)
            kv, qg, hd = items[it]
            sb_ = (n % 2) * 2
            pr_ = n % 3
            qsl = qT[:, hd, qg * 512:(qg + 1) * 512]
            P.add("pe", mmgroup([(banks[sb_ + j][:, :], kT[:, kv, (s2 * 2 + j) * 128:(s2 * 2 + j + 1) * 128], qsl, True, True)
                                 for j in range(2)]),
                  reads=["akT", "aqT"], writes=[BN[sb_], BN[sb_ + 1]])
            P.add("act", (lambda pr_, sb_: lambda e: e.activation(pt[pr_], big[sb_ // 2][:, :],
                                                                   AF.Exp, bias=negB[:, 2:3], scale=scale))(pr_, sb_),
                  reads=[BN[sb_], BN[sb_ + 1], "anegB"], writes=[("apt%d" % pr_, 0), ("apt%d" % pr_, 1)])

        def emit_PV(n):
            it, s2 = divmod(n, 8)
            kv, qg, hd = items[it]
            pr_ = n % 3
            ob = 4 + it % 2
            db = 6 + it % 2
            its = []
            for j in range(2):
                sc = s2 * 2 + j
                its.append((banks[ob][:, :], vv[:, sc, kv * 128:(kv + 1) * 128], pt[pr_][:, j * 512:(j + 1) * 512], sc == 0, sc == 15))
                its.append((banks[db][:, :], ones_b, pt[pr_][:, j * 512:(j + 1) * 512], sc == 0, sc == 15))
            P.add("pe", mmgroup(its), reads=["avv", ("apt%d" % pr_, 0), ("apt%d" % pr_, 1), "c:cbf"], writes=[BN[ob], BN[db]])
            if s2 == 7:
                rr = it % 2
                P.add("dve", (lambda rr, db: lambda e: e.reciprocal(rden[rr], banks[db][:, :]))(rr, db),
                      reads=[BN[db]], writes=["arden%d" % rr])
                P.add("dve", (lambda rr, ob, hd, qg: lambda e: e.tensor_tensor(
                    oT[:, hd, qg * 512:(qg + 1) * 512], banks[ob][:, :], rden[rr], ALU.mult))(rr, ob, hd, qg),
                      reads=[BN[ob], "arden%d" % rr], writes=[("aoT", hd, qg)])

        for n in range(NS + LOOK):
            if n < NS:
                emit_S(n)
            if n >= LOOK:
                emit_PV(n - LOOK)
        sq = [A.get([512], BF16) for _ in range(2)]
        rb = A.get([L], F32)
        mst = [A.get([L], BF16) for _ in range(2)]
        for qg in range(4):
            bi = 7
            for hd in range(8):
                r = hd % 2
                P.add("act", (lambda r, hd, qg: lambda e: e.activation(sq[r], oT[:, hd, qg * 512:(qg + 1) * 512],
                                                                        AF.Square))(r, hd, qg),
                      reads=[("aoT", hd, qg)], writes=["asq%d" % r])
                P.add("pe", mmgroup([(banks[bi][:, :], ones_b, sq[r], hd == 0, hd == 7)]), reads=["asq%d" % r, "c:cbf"],
                      writes=[BN[bi]])
            P.add("act", (lambda qg, bi: lambda e: e.activation(rb[:, qg * 512:(qg + 1) * 512], banks[bi][:, :], AF.Ln,
                                                                bias=EPS, scale=1.0 / 1024))(qg, bi),
                  reads=[BN[bi]], writes=[("arb", qg)])
            P.add("act", (lambda qg: lambda e: e.activation(rb[:, qg * 512:(qg + 1) * 512], rb[:, qg * 512:(qg + 1) * 512],
                                                            AF.Exp, scale=-0.5))(qg),
                  reads=[("arb", qg)], writes=[("arb", qg)])
        for hd in range(8):
            r = hd % 2
            P.add("dve", (lambda r, hd: lambda e: e.scalar_tensor_tensor(
                mst[r], oT[:, hd, :], pcol[:, pc + 48 + hd:pc + 49 + hd], rb, ALU.mult, ALU.mult))(r, hd),
                  reads=[("aoT", hd, qg) for qg in range(4)] + [("arb", qg) for qg in range(4)] + ["c:pcol"],
                  writes=["amst%d" % r])
            P.dma("sp", "amst%d" % r, MIXT[s][hd], mst[r], reads=["amst%d" % r], writes=[("MIXT", s, hd)])

    def outproj_phase(s, l):
        P.barrier()
        A.reset()
        mT2 = [A.get([16, 1024], BF16) for _ in range(2)]
        hres = [A.get([1024], F32) for _ in range(3)]
        for t in range(2):
            P.dma("sp", "omT%d" % t, mT2[t], MIXT[s][:, :, t * 1024:(t + 1) * 1024].rearrange("k p t -> p k t"),
                  reads=[("MIXT", s, k) for k in range(8)] + [("MIXT", s, "s", g, c) for g in range(2) for c in range(16)],
                  writes=["omT%d" % t])
        for t in range(2):
            t0 = t * 1024
            mT = mT2[t]
            for m in range(16):
                sl = wslot_gu()
                P.dma("pool", "w:gu%d" % sl, wgu_sb[sl][:, 0:2048], wout_d[l, m], writes=["wgu%d" % sl])
                wv = wgu_sb[sl][:, 0:2048].rearrange("p (k c) -> p k c", k=16)
                hs = m % 3
                P.dma("sp", "ohres%d" % hs, hres[hs], HT[s][m, :, t0:t0 + 1024], reads=[("HT", s, m)],
                      writes=["ohres%d" % hs])
                for half in range(2):
                    bi = (m % 2) * 2 + half
                    P.add("pe", mmgroup([(banks[bi][:, :], wv[:, k, :], mT[:, k, half * 512:(half + 1) * 512], k == 0, k == 15)
                                         for k in range(16)]), reads=["wgu%d" % sl, "omT%d" % t], writes=[BN[bi]])
                    P.add("dve", (lambda hs, half, bi: lambda e: e.tensor_tensor(
                        hres[hs][:, half * 512:(half + 1) * 512], banks[bi][:, :],
                        hres[hs][:, half * 512:(half + 1) * 512], ALU.add))(hs, half, bi),
                          reads=[BN[bi], "ohres%d" % hs], writes=["ohres%d" % hs])
                P.dma("sp", "ohst%d" % hs, HT[s][m, :, t0:t0 + 1024], hres[hs], reads=["ohres%d" % hs],
                      writes=[("HT", s, m)])

    def final_phase(s):
        gc0 = depth * NPC_L
        P.barrier()
        A.reset()
        hh2 = [A.get([16, 512], F32) for _ in range(2)]
        sq = [A.get([512], BF16) for _ in range(2)]
        rb2 = [A.get([512], F32) for _ in range(2)]
        ost = [A.get([D], F32) for _ in range(2)]

        def load(tg):
            hs = tg % 2
            P.dma("sp", "zhh%d" % hs, hh2[hs], HT[s][:, :, tg * 512:(tg + 1) * 512].rearrange("k p t -> p k t"),
                  reads=[("HT", s, k) for k in range(16)],
                  writes=["zhh%d" % hs] + [("zhn%d" % hs, k) for k in range(16)])

        load(0)
        load(1)
        for tg in range(4):
            hs = tg % 2
            hh = hh2[hs]
            rb = rb2[hs]
            sb_ = 6 + hs
            for k in range(16):
                r = k % 2
                P.add("act", (lambda r, k, hh: lambda e: e.activation(sq[r], hh[:, k, :], AF.Square))(r, k, hh),
                      reads=["zhh%d" % hs], writes=["zsq%d" % r])
                P.add("pe", mmgroup([(banks[sb_][:, :], ones_b, sq[r], k == 0, k == 15)]), reads=["zsq%d" % r, "c:cbf"],
                      writes=[BN[sb_]])
            P.add("act", (lambda rb, sb_: lambda e: e.activation(rb, banks[sb_][:, :], AF.Ln, bias=EPS, scale=1.0 / D))(rb, sb_),
                  reads=[BN[sb_]], writes=["zrb%d" % hs])
            P.add("act", (lambda rb: lambda e: e.activation(rb, rb, AF.Exp, scale=-0.5))(rb), reads=["zrb%d" % hs],
                  writes=["zrb%d" % hs])
            for k in range(16):
                P.add("dve", (lambda k, hh, rb: lambda e: e.scalar_tensor_tensor(hh[:, k, :], hh[:, k, :],
                                                                                 pcol[:, gc0 + k:gc0 + k + 1], rb, ALU.mult,
                                                                                 ALU.mult))(k, hh, rb),
                      reads=["zhh%d" % hs, "zrb%d" % hs, "c:pcol"], writes=[("zhn%d" % hs, k)])
            for c4 in range(4):
                o = ost[c4 % 2]
                for k4 in range(4):
                    bi = (c4 * 4 + k4) % 4
                    bv = banks[bi][:, :].rearrange("p (a b) -> p a b", a=4)
                    P.add("pe", trgroup([(bv[:, i, :], hh[:, k4 * 4 + i, c4 * 128:(c4 + 1) * 128], ident_f) for i in range(4)]),
                          reads=[("zhn%d" % hs, k4 * 4 + i) for i in range(4)] + ["c:cmat"], writes=[BN[bi]])
                    dst = o[:, k4 * 512:(k4 + 1) * 512]
                    if k4 % 2 == 0:
                        P.add("dve", (lambda dst, bi: lambda e: e.tensor_copy(dst, banks[bi][:, :]))(dst, bi),
                              reads=[BN[bi]], writes=[("zost%d" % (c4 % 2), k4)])
                    else:
                        P.add("act", (lambda dst, bi: lambda e: e.copy(dst, banks[bi][:, :]))(dst, bi),
                              reads=[BN[bi]], writes=[("zost%d" % (c4 % 2), k4)])
                c = tg * 4 + c4
                op = P.dma("sp", "zost%d" % (c4 % 2), out_d[s, c * 128:(c + 1) * 128, :], o,
                           reads=[("zost%d" % (c4 % 2), k4) for k4 in range(4)], writes=[("OUT", s, c)])
                final_ops.append(op)
            if tg + 2 < 4:
                load(tg + 2)

    done = False
    for s in range(nseq):
        if done:
            break
        init_phase(s)
        if stop_after == "init":
            done = True
            break
        for l in range(depth):
            steps = [("ffn1", lambda: ffn_phase(s, l, 0)), ("inproj", lambda: inproj_phase(s, l)),
                     ("ssd", lambda: ssd_phase(s, l)), ("attn", lambda: attn_phase(s, l)),
                     ("outproj", lambda: outproj_phase(s, l)), ("ffn2", lambda: ffn_phase(s, l, 1))]
            for name, fn in steps:
                fn()
                if stop_after == name and l == 0:
                    done = True
                    break
            if done:
                break
        if not done:
            final_phase(s)
    if done:
        final_ops.extend([o for k, o in P.dma_last.items() if not str(k).startswith("w:") and not str(k).startswith("c")])
    P.emit(final_wait_ops=final_ops)
    es.close()
    return nc, A.peak


def _rope_tables():
    rows = L // 64
    row_pos = np.repeat(np.arange(rows), 64).astype(np.float32)
    col_pos = np.tile(np.arange(64), rows).astype(np.float32)
    inv_freq = (np.float32(10000.0) ** (-np.arange(0, 64, 2, dtype=np.float32) / np.float32(64))).astype(np.float32)
    ang_r = row_pos[:, None] * inv_freq[None, :]
    ang_c = col_pos[:, None] * inv_freq[None, :]
    ang = np.concatenate([ang_r, ang_r, ang_c, ang_c], axis=1)
    cos = np.cos(ang).astype(np.float32).T
    sin = np.sin(ang).astype(np.float32).T
    return np.ascontiguousarray(np.stack([cos, sin], axis=1))


def _cmat():
    s = np.arange(128)[:, None]
    l = np.arange(128)[None, :]
    ident = np.eye(128, dtype=np.float32)
    triu = (s <= l).astype(np.float32)
    tril = (s >= l).astype(np.float32)
    negf = np.where(l >= s, 0.0, -30000.0).astype(np.float32)
    negb = np.where(l <= s, 0.0, -30000.0).astype(np.float32)
    ones = np.ones((128, 128), np.float32)
    rot = np.zeros((128, 128), np.float32)
    for m in range(128):
        if (m % 64) < 32:
            rot[m + 32, m] = -1.0
        else:
            rot[m - 32, m] = 1.0
    return np.ascontiguousarray(np.concatenate([ident, triu, tril, negf, negb, ones, rot], axis=1))


def prep_shared(inp, depth=DEPTH):
    f = lambda a: np.asarray(a, dtype=np.float32)
    out = {}
    for i, (gu, dn) in enumerate([("ffn1_w_gu", "ffn1_w_down"), ("ffn2_w_gu", "ffn2_w_down")]):
        w = f(inp[gu])[:depth].reshape(depth, 16, 128, 2, NJ, 128)
        out["wgu%d" % (i + 1)] = np.ascontiguousarray(w.transpose(0, 4, 2, 1, 3, 5)).reshape(depth, NJ, 128, 4096)
        w = f(inp[dn])[:depth].reshape(depth, NJ, 128, 16, 128)
        out["wd%d" % (i + 1)] = np.ascontiguousarray(w.transpose(0, 3, 2, 1, 4)).reshape(depth, 16, 128, DFF)
    w = f(inp["w_in"])[:depth].reshape(depth, 16, 128, 4128).transpose(0, 2, 1, 3)
    parts = [np.ascontiguousarray(w[:, :, :, c0:c0 + n]).reshape(depth, 128, 16 * n) for (_, c0, n) in WIN_BLOCKS]
    out["win"] = np.ascontiguousarray(np.concatenate(parts, axis=2))
    w = f(inp["w_out"])[:depth].reshape(depth, 16, 128, 16, 128)
    out["wout"] = np.ascontiguousarray(w.transpose(0, 3, 2, 1, 4)).reshape(depth, 16, 128, 2048)
    pcol = np.zeros((128, depth * NPC_L + 16), np.float32)
    prow = np.zeros((1, depth * NPR_L), np.float32)
    colv = lambda v: f(v).reshape(-1, 128).T
    for l in range(depth):
        b = l * NPC_L
        pcol[:, b:b + 16] = colv(inp["ffn1_norm"][l])
        pcol[:, b + 16:b + 32] = colv(inp["mix_norm"][l])
        pcol[:, b + 32:b + 48] = colv(inp["ffn2_norm"][l])
        pcol[:, b + 48:b + 56] = colv(inp["attn_out_norm"][l])
        pcol[:, b + 56] = f(inp["q_norm"][l])
        pcol[:, b + 57] = f(inp["k_norm"][l])
        cw = f(inp["conv_w"][l])
        for bb in range(12):
            for k in range(5):
                pcol[:, b + 58 + bb * 5 + k] = cw[k, bb * 128:(bb + 1) * 128]
        pcol[:, b + 118:b + 130] = colv(inp["conv_b"][l])
        r = l * NPR_L
        prow[0, r:r + 1024] = f(inp["ssd_out_norm"][l])
        prow[0, r + 1024:r + 1056] = f(inp["dt_bias"][l]).reshape(-1)
        prow[0, r + 1056:r + 1088] = f(inp["a_log"][l]).reshape(-1)
        prow[0, r + 1088:r + 1104] = f(inp["d_skip"][l])
        prow[0, r + 1104:r + 1232] = f(inp["q_norm"][l])
        prow[0, r + 1232:r + 1360] = f(inp["k_norm"][l])
    pcol[:, depth * NPC_L:depth * NPC_L + 16] = colv(inp["final_norm"])
    out["pcol"] = pcol
    out["prow"] = prow
    out["cmat"] = _cmat()
    out["rope"] = _rope_tables()
    return out


_CACHE = {}


def kernel(**inputs):
    x = np.asarray(inputs["x"], dtype=np.float32)
    shared = prep_shared(inputs)
    if "nc" not in _CACHE:
        _CACHE["nc"] = build()[0]
    nc = _CACHE["nc"]
    in_maps = []
    for c in range(NCORES):
        m = dict(shared)
        m["x"] = np.ascontiguousarray(x[c * 2:(c + 1) * 2])
        in_maps.append(m)
    res = run_bass_kernel_spmd(nc, in_maps, core_ids=list(range(NCORES)))
    return np.concatenate([r["out"] for r in res.results], axis=0).astype(np.float32)
```

```python
import numpy as np
import concourse.bass as bass
import concourse.mybir as mybir
from concourse.bass_utils import run_bass_kernel_spmd
from contextlib import ExitStack

F32 = mybir.dt.float32
BF16 = mybir.dt.bfloat16
AF = mybir.ActivationFunctionType
ALU = mybir.AluOpType
AX = mybir.AxisListType

ENGS = ("pe", "act", "dve", "pool", "sp")
NCORES = 8
D = 2048
L = 2048
DFF = 5632
NJ = 44
DEPTH = 4
EPS = 1e-6
NPC_L = 130
NPR_L = 1360
WIN_BLOCKS = ([("q%d" % i, i * 128, 128) for i in range(8)] + [("k%d" % i, 1024 + i * 128, 128) for i in range(2)]
              + [("v", 1280, 256)] + [("z%d" % i, 1536 + i * 256, 256) for i in range(4)]
              + [("x%d" % i, 2560 + i * 128, 128) for i in range(12)] + [("dt", 4096, 32)])
WIN_OFF = {}
_o = 0
for _n, _c0, _nc in WIN_BLOCKS:
    WIN_OFF[_n] = (_o, _nc)
    _o += 16 * _nc
WIN_TOT = _o


class Op:
    __slots__ = ("eng", "fn", "deps", "signaled", "sem", "val", "is_dma")

    def __init__(self, eng, fn, is_dma):
        self.eng = eng
        self.fn = fn
        self.deps = []
        self.signaled = False
        self.sem = None
        self.val = 0
        self.is_dma = is_dma


class Prog:
    def __init__(self, nc, es, self_sync=True):
        self.nc = nc
        self.es = es
        self.ops = {e: [] for e in ENGS}
        self.last_write = {}
        self.readers = {}
        self.dma_last = {}
        self.dma_cnt = {}
        self.dma_sems = {}
        self.eng_sems = {}
        self.self_sync = self_sync
        self.pending = {}
        self.nops = 0

    def sb(self, name, shape, dt):
        return self.es.enter_context(self.nc.sbuf_tensor("sb_" + name, list(shape), dt))

    def ps(self, name, shape, dt):
        return self.es.enter_context(self.nc.psum_tensor(name, list(shape), dt))

    def _deps(self, op, reads, writes):
        deps = []
        lw = self.last_write
        rd = self.readers
        for r in reads:
            o = lw.get(r)
            if o is not None:
                deps.append(o)
        for w in writes:
            o = lw.get(w)
            if o is not None:
                deps.append(o)
            l = rd.get(w)
            if l:
                deps.extend(l)
        for w in writes:
            lw[w] = op
            rd[w] = []
        for r in reads:
            if isinstance(r, str) and r.startswith("c:"):
                continue
            l = rd.get(r)
            if l is None:
                rd[r] = [op]
            else:
                l.append(op)
        if not (op.is_dma and op.eng == "pool"):
            p = self.pending.pop(op.eng, None)
            if p:
                deps.extend(p)
        return deps

    def barrier(self):
        lst = []
        for e in ("pe", "act", "dve"):
            if self.ops[e]:
                lst.append(self.ops[e][-1])
        for o in reversed(self.ops["pool"]):
            if not o.is_dma:
                lst.append(o)
                break
        for k, o in self.dma_last.items():
            if not str(k).startswith("w:"):
                lst.append(o)
        for e in ("pe", "act", "dve", "sp", "pool"):
            self.pending[e] = list(lst)
        self.last_write = {k: v for k, v in self.last_write.items() if _persist(k)}
        self.readers = {k: v for k, v in self.readers.items() if _persist(k)}

    def add(self, eng, fn, reads=(), writes=()):
        op = Op(eng, fn, False)
        deps = self._deps(op, reads, writes)
        seen = set()
        for d in deps:
            if d is op or id(d) in seen:
                continue
            seen.add(id(d))
            if d.eng == eng and not d.is_dma:
                if eng == "pe" or not self.self_sync:
                    continue
            op.deps.append(d)
            d.signaled = True
        self.ops[eng].append(op)
        self.nops += 1
        return op

    def dma(self, queue, key, out, in_, reads=(), writes=(), **kw):
        def fn(e):
            return e.dma_start(out=out, in_=in_, **kw)
        op = Op(queue, fn, True)
        deps = self._deps(op, reads, writes)
        prev = self.dma_last.get(key)
        if prev is not None:
            deps.append(prev)
        seen = set()
        for d in deps:
            if d is op or id(d) in seen:
                continue
            seen.add(id(d))
            op.deps.append(d)
            d.signaled = True
        self.dma_last[key] = op
        n = self.dma_cnt.get(key, 0) + 1
        self.dma_cnt[key] = n
        op.sem = key
        op.val = 16 * n
        op.signaled = True
        self.ops[queue].append(op)
        self.nops += 1
        return op

    def emit(self, final_wait_ops=()):
        nc = self.nc
        es = self.es
        for i, key in enumerate(self.dma_cnt):
            self.dma_sems[key] = es.enter_context(nc.semaphore("d%d" % i))
        for e in ENGS:
            self.eng_sems[e] = es.enter_context(nc.semaphore("e_" + e))
        for e in ENGS:
            t = 0
            for op in self.ops[e]:
                if op.is_dma:
                    op.sem = self.dma_sems[op.sem]
                elif op.signaled:
                    t += 1
                    op.sem = self.eng_sems[e]
                    op.val = t
        block = es.enter_context(nc.Block())
        engmap = {"pe": block.tensor, "act": block.scalar, "dve": block.vector,
                  "pool": block.gpsimd, "sp": block.sync}
        final_wait_ops = list(final_wait_ops)

        def make(ename):
            ops = self.ops[ename]

            def body(e):
                seen = {}
                for op in ops:
                    for d in op.deps:
                        s = d.sem
                        k = id(s)
                        if seen.get(k, 0) < d.val:
                            e.wait_ge(s, d.val)
                            seen[k] = d.val
                    inst = op.fn(e)
                    if op.signaled:
                        inst.then_inc(op.sem, 16 if op.is_dma else 1)
                if ename == "sp":
                    for d in final_wait_ops:
                        if seen.get(id(d.sem), 0) < d.val:
                            e.wait_ge(d.sem, d.val)
                            seen[id(d.sem)] = d.val
            return body

        for ename in ENGS:
            if self.ops[ename] or (ename == "sp" and final_wait_ops):
                engmap[ename](make(ename))


def _persist(k):
    if isinstance(k, tuple):
        return k[0] in ("HT", "QT", "KT", "V", "Z", "XBCT", "DT", "MIXT", "OUT", "CST")
    return k.startswith("c:") or k.startswith("w") or k.startswith("ps")


class Arena:
    def __init__(self, base, nwords):
        self.base = base
        self.nwords = nwords
        self.off = 0
        self.peak = 0

    def reset(self):
        self.off = 0

    def mark(self):
        return self.off

    def release(self, m):
        self.off = m

    def get(self, free_shape, dt):
        n = 1
        for s in free_shape:
            n *= s
        words = n if dt == F32 else (n + 1) // 2
        words = (words + 15) // 16 * 16
        assert self.off + words <= self.nwords, ("arena overflow", self.off, words, self.nwords)
        v = self.base[:, self.off:self.off + words]
        self.off += words
        self.peak = max(self.peak, self.off)
        if dt != F32:
            v = v.bitcast(dt)
        v = v[:, 0:n]
        if len(free_shape) == 2:
            v = v.rearrange("p (a b) -> p a b", a=free_shape[0])
        elif len(free_shape) == 3:
            v = v.rearrange("p (a b c) -> p a b c", a=free_shape[0], b=free_shape[1])
        return v


def mmgroup(items):
    def fn(e):
        inst = None
        for (o, l, r, s, t) in items:
            inst = e.matmul(o, l, r, start=s, stop=t)
        return inst
    return fn


def trgroup(items):
    def fn(e):
        inst = None
        for (o, i, ident) in items:
            inst = e.transpose(o, i, ident)
        return inst
    return fn


def build(depth=DEPTH, nseq=2, dbg=(), stop_after=None):
    nc = bass.Bass("TRN2", target_bir_lowering=False)

    def din(name, shape, dt=F32):
        return nc.dram_tensor(name, list(shape), dt, kind="ExternalInput").ap()

    def dscr(name, shape, dt):
        if name in dbg:
            return nc.dram_tensor(name, list(shape), dt, kind="ExternalOutput").ap()
        return nc.dram_tensor(name, list(shape), dt).ap()

    x_d = din("x", [nseq, L, D])
    wgu_d = [din("wgu1", [depth, NJ, 128, 4096]), din("wgu2", [depth, NJ, 128, 4096])]
    wd_d = [din("wd1", [depth, 16, 128, DFF]), din("wd2", [depth, 16, 128, DFF])]
    win_d = din("win", [depth, 128, WIN_TOT])
    wout_d = din("wout", [depth, 16, 128, 2048])
    pcol_d = din("pcol", [128, depth * NPC_L + 16])
    prow_d = din("prow", [1, depth * NPR_L])
    cmat_d = din("cmat", [128, 7 * 128])
    rope_d = din("rope", [128, 2, L])
    out_d = nc.dram_tensor("out", [nseq, L, D], F32, kind="ExternalOutput").ap()

    HT = [dscr("HT%d" % s, [16, 128, L], F32) for s in range(nseq)]
    QT = [dscr("QT%d" % s, [8, 128, L], BF16) for s in range(nseq)]
    KT = [dscr("KT%d" % s, [2, 128, L], BF16) for s in range(nseq)]
    VV = [dscr("V%d" % s, [16, 128, 256], BF16) for s in range(nseq)]
    ZZ = [dscr("Z%d" % s, [16, 128, 1024], BF16) for s in range(nseq)]
    XBCT = [dscr("XBCT%d" % s, [12, 128, L], F32) for s in range(nseq)]
    DTS = [dscr("DT%d" % s, [16, 128, 32], F32) for s in range(nseq)]
    MIXT = [dscr("MIXT%d" % s, [16, 128, L], BF16) for s in range(nseq)]
    CST = [dscr("CST%d" % s, [16, 32, 128], F32) for s in range(nseq)]

    es = ExitStack()
    P = Prog(nc, es)
    pcol = P.sb("pcol", [128, depth * NPC_L + 16], F32)
    cmat = P.sb("cmat", [128, 7 * 128], F32)
    cbf = P.sb("cbf", [128, 3 * 128], BF16)
    wgu_sb = [P.sb("wgu%d" % i, [128, 4096], BF16) for i in range(3)]
    wd_sb = [P.sb("wd%d" % i, [128, DFF], BF16) for i in range(2)]
    ARENA_W = 38 * 1024
    arena_t = P.sb("arena", [128, ARENA_W], F32)
    A = Arena(arena_t, ARENA_W)
    big = [P.ps("psb%d" % i, [128, 1024], F32) for i in range(4)]
    banks = [big[i // 2][:, (i % 2) * 512:(i % 2 + 1) * 512] for i in range(8)]
    BN = ["ps%d" % i for i in range(8)]

    ident_f = cmat[:, 0:128]
    triu_f = cmat[:, 128:256]
    tril_f = cmat[:, 256:384]
    neg_fb = cmat[:, 384:640]
    ones_f = cmat[:, 640:768]
    ident_b = cbf[:, 0:128]
    ones_b = cbf[:, 128:256]
    rot_b = cbf[:, 256:384]

    P.dma("sp", "c0", pcol[:], pcol_d, writes=["c:pcol"])
    P.dma("sp", "c1", cmat[:], cmat_d, writes=["c:cmat"])
    P.add("dve", lambda e: e.tensor_copy(cbf[:, 0:128], cmat[:, 0:128]), reads=["c:cmat"], writes=["c:cbf"])
    P.add("dve", lambda e: e.tensor_copy(cbf[:, 128:384], cmat[:, 640:896]), reads=["c:cmat"], writes=["c:cbf"])

    cnt = {"gu": 0, "wd": 0}
    final_ops = []

    def wslot_gu():
        s = cnt["gu"] % 3
        cnt["gu"] += 1
        return s

    def wslot_d():
        s = cnt["wd"] % 2
        cnt["wd"] += 1
        return s

    def bf_bank(i, shape):
        v = banks[i][:, :].bitcast(BF16)
        if len(shape) == 2:
            return v[:, 0:shape[0] * shape[1]].rearrange("p (a b) -> p a b", a=shape[0])
        return v[:, 0:shape[0]]

    def norm_to_uT(tag, s, t0, T, gc0, uT, nfeat_chunks=16, inv_n=1.0 / D, hin=None, hn=None, defer=False):
        nh = T // 512
        if hin is None:
            hin = [A.get([T], F32) for _ in range(3)]
        if hn is None:
            hn = [tag + "hin%d" % i for i in range(len(hin))]
        NS_ = len(hin)
        sq = [A.get([T], BF16) for _ in range(NS_)]
        rb = A.get([T], F32)
        for k in range(nfeat_chunks):
            sl = k % NS_
            P.dma("sp", hn[sl], hin[sl], HT[s][k, :, t0:t0 + T],
                  reads=[("HT", s, k)], writes=[hn[sl]])
            P.add("act", (lambda sl: lambda e: e.activation(sq[sl], hin[sl], AF.Square))(sl),
                  reads=[hn[sl]], writes=[tag + "sq%d" % sl])
            P.add("pe", mmgroup([(banks[4 + h][:, :], ones_b, sq[sl][:, h * 512:(h + 1) * 512], k == 0,
                                  k == nfeat_chunks - 1) for h in range(nh)]),
                  reads=[tag + "sq%d" % sl, "c:cbf"], writes=[BN[4 + h] for h in range(nh)])
            if defer:
                P.add("dve", (lambda sl, k: lambda e: e.tensor_scalar(uT[:, k, :], hin[sl], pcol[:, gc0 + k:gc0 + k + 1], 0.0,
                                                                      ALU.mult, ALU.add))(sl, k),
                      reads=[hn[sl], "c:pcol"], writes=[(tag + "uT", k)])
        for h in range(nh):
            P.add("act", (lambda h: lambda e: e.activation(rb[:, h * 512:(h + 1) * 512], banks[4 + h][:, :], AF.Ln,
                                                            bias=EPS, scale=inv_n))(h),
                  reads=[BN[4 + h]], writes=[tag + "rb"])
        P.add("act", lambda e: e.activation(rb, rb, AF.Exp, scale=-0.5), reads=[tag + "rb"], writes=[tag + "rb"])
        if defer:
            return rb
        for k in range(nfeat_chunks):
            sl = k % NS_
            P.dma("sp", hn[sl], hin[sl], HT[s][k, :, t0:t0 + T],
                  reads=[("HT", s, k)], writes=[hn[sl]])
            P.add("dve", (lambda sl, k: lambda e: e.scalar_tensor_tensor(uT[:, k, :], hin[sl], pcol[:, gc0 + k:gc0 + k + 1],
                                                                       rb, ALU.mult, ALU.mult))(sl, k),
                  reads=[hn[sl], tag + "rb", "c:pcol"], writes=[(tag + "uT", k)])

    def init_phase(s):
        P.barrier()
        A.reset()
        xin = [A.get([D], F32) for _ in range(2)]
        stg = [A.get([16, 512], F32) for _ in range(2)]
        for tg in range(4):
            st = stg[tg % 2]
            for c4 in range(4):
                c = tg * 4 + c4
                sl = c % 2
                P.dma("sp", "ixin%d" % sl, xin[sl], x_d[s, c * 128:(c + 1) * 128, :], writes=["ixin%d" % sl])
                for k4 in range(4):
                    bi = (c * 4 + k4) % 4
                    bv = banks[bi][:, :].rearrange("p (a b) -> p a b", a=4)
                    P.add("pe", trgroup([(bv[:, i, :], xin[sl][:, (k4 * 4 + i) * 128:(k4 * 4 + i + 1) * 128], ident_f)
                                         for i in range(4)]),
                          reads=["ixin%d" % sl, "c:cmat"], writes=[BN[bi]])
                    eng = "dve" if k4 % 2 == 0 else "act"
                    dst = st[:, k4 * 4:(k4 + 1) * 4, c4 * 128:(c4 + 1) * 128]
                    if eng == "dve":
                        P.add("dve", (lambda dst, bv: lambda e: e.tensor_copy(dst, bv))(dst, bv),
                              reads=[BN[bi]], writes=[("istg%d" % (tg % 2), c4, k4)])
                    else:
                        P.add("act", (lambda dst, bv: lambda e: e.copy(dst, bv))(dst, bv),
                              reads=[BN[bi]], writes=[("istg%d" % (tg % 2), c4, k4)])
            P.dma("sp", "istg%d" % (tg % 2), HT[s][:, :, tg * 512:(tg + 1) * 512].rearrange("k p t -> p k t"), st,
                  reads=[("istg%d" % (tg % 2), c4, k4) for c4 in range(4) for k4 in range(4)],
                  writes=[("HT", s, k) for k in range(16)])

    def ffn_phase(s, l, which, chain_prev=False, next_gc=None):
        gc0 = l * NPC_L + (0 if which == 0 else 32)
        tag = "f"
        if not chain_prev:
            P.barrier()
        A.reset()
        uT = A.get([16, 1024], BF16)
        aT = A.get([NJ, 1024], BF16)
        hres = [A.get([1024], F32) for _ in range(3)]
        sq = [A.get([1024], BF16) for _ in range(3)]
        rb = A.get([1024], F32)
        sgbuf = A.get([4, 512], F32)
        sg = [sgbuf[:, 0, :], sgbuf[:, 1, :]]
        sgi = [sgbuf[:, 2, :], sgbuf[:, 3, :]]
        hinB = [sgbuf[:, 0:2, :].rearrange("p a b -> p (a b)"), sgbuf[:, 2:4, :].rearrange("p a b -> p (a b)")]
        hinB_n = [["fsg0", "fsg1"], ["fsgi0", "fsgi1"]]

        def norm_load(k, t0, hin_ap, names, key):
            P.dma("sp", key, hin_ap, HT[s][k, :, t0:t0 + 1024], reads=[("HT", s, k)], writes=names)

        def norm_sq_ut(k, hin_ap, names, sqi, gc=None):
            gc = gc0 if gc is None else gc
            P.add("act", (lambda: lambda e: e.activation(sq[sqi], hin_ap, AF.Square))(), reads=names,
                  writes=["fsq%d" % sqi])
            P.add("dve", (lambda: lambda e: e.tensor_scalar(uT[:, k, :], hin_ap, pcol[:, gc + k:gc + k + 1], 0.0,
                                                            ALU.mult, ALU.add))(),
                  reads=names + ["c:pcol"], writes=[(tag + "uT", k)])

        def norm_mm(k, sqi, b0):
            P.add("pe", mmgroup([(banks[b0 + h][:, :], ones_b, sq[sqi][:, h * 512:(h + 1) * 512], k == 0, k == 15)
                                 for h in range(2)]),
                  reads=["fsq%d" % sqi, "c:cbf"], writes=[BN[b0], BN[b0 + 1]])

        def norm_fin(b0):
            for h in range(2):
                P.add("act", (lambda h: lambda e: e.activation(rb[:, h * 512:(h + 1) * 512], banks[b0 + h][:, :], AF.Ln,
                                                                bias=EPS, scale=1.0 / D))(h),
                      reads=[BN[b0 + h]], writes=["frb"])
            P.add("act", lambda e: e.activation(rb, rb, AF.Exp, scale=-0.5), reads=["frb"], writes=["frb"])

        if not chain_prev:
            for k in range(16):
                sl = k % 3
                norm_load(k, 0, hres[sl], ["fhres%d" % sl], "fhres%d" % sl)
                norm_sq_ut(k, hres[sl], ["fhres%d" % sl], sl)
                norm_mm(k, sl, 4)
            norm_fin(4)
        for t in range(2):
            t0 = t * 1024
            ov = (t == 0) or (next_gc is not None)
            ov_t0 = 1024 if t == 0 else 0
            ov_gc = gc0 if t == 0 else next_gc
            for j in range(NJ):
                sl = wslot_gu()
                P.dma("pool", "w:gu%d" % sl, wgu_sb[sl][:, :], wgu_d[which][l, j], writes=["wgu%d" % sl])
                wv = wgu_sb[sl][:, :].rearrange("p (k c) -> p k c", k=16)
                for half in range(2):
                    i = j * 2 + half
                    gb, ub = (i % 2) * 2, (i % 2) * 2 + 1
                    rhs = lambda k: uT[:, k, half * 512:(half + 1) * 512]
                    P.add("pe", mmgroup([(banks[gb][:, :], wv[:, k, 0:128], rhs(k), k == 0, k == 15) for k in range(16)]),
                          reads=["wgu%d" % sl] + [(tag + "uT", k) for k in range(16)], writes=[BN[gb]])
                    P.add("pe", mmgroup([(banks[ub][:, :], wv[:, k, 128:256], rhs(k), k == 0, k == 15) for k in range(16)]),
                          reads=["wgu%d" % sl] + [(tag + "uT", k) for k in range(16)], writes=[BN[ub]])
                    rbh = rb[:, half * 512:(half + 1) * 512]
                    P.add("dve", (lambda i, gb, rbh: lambda e: e.tensor_tensor(sgi[i % 2], banks[gb][:, :], rbh, ALU.mult))(
                        i, gb, rbh), reads=[BN[gb], "frb"], writes=["fsgi%d" % (i % 2)])
                    P.add("act", (lambda i: lambda e: e.activation(sg[i % 2], sgi[i % 2], AF.Silu))(i),
                          reads=["fsgi%d" % (i % 2)], writes=["fsg%d" % (i % 2)])
                    P.add("dve", (lambda i, ub, rbh: lambda e: e.tensor_tensor(sgi[i % 2], banks[ub][:, :], rbh, ALU.mult))(
                        i, ub, rbh), reads=[BN[ub], "frb", "fsg%d" % (i % 2)], writes=["fsgi%d" % (i % 2)])
                    P.add("dve", (lambda i, j, half: lambda e: e.tensor_tensor(
                        aT[:, j, half * 512:(half + 1) * 512], sg[i % 2], sgi[i % 2], ALU.mult))(i, j, half),
                          reads=["fsg%d" % (i % 2), "fsgi%d" % (i % 2)], writes=[("faT", j, half)])
            pend = None
            for m in range(16):
                sl = wslot_d()
                P.dma("pool", "w:d%d" % sl, wd_sb[sl][:, :], wd_d[which][l, m], writes=["wd%d" % sl])
                wv = wd_sb[sl][:, :].rearrange("p (j c) -> p j c", j=NJ)
                hs = m % 3
                P.dma("sp", "fhres%d" % hs, hres[hs], HT[s][m, :, t0:t0 + 1024],
                      reads=[("HT", s, m)], writes=["fhres%d" % hs])
                for half in range(2):
                    bi = 4 + (m % 2) * 2 + half
                    P.add("pe", mmgroup([(banks[bi][:, :], wv[:, j, :], aT[:, j, half * 512:(half + 1) * 512], j == 0,
                                          j == NJ - 1) for j in range(NJ)]),
                          reads=["wd%d" % sl] + [("faT", j, half) for j in range(NJ)], writes=[BN[bi]])
                    P.add("dve", (lambda hs, half, bi: lambda e: e.scalar_tensor_tensor(
                        hres[hs][:, half * 512:(half + 1) * 512], banks[bi][:, :], 0.5,
                        hres[hs][:, half * 512:(half + 1) * 512], ALU.mult, ALU.add))(hs, half, bi),
                          reads=[BN[bi], "fhres%d" % hs], writes=["fhres%d" % hs])
                if ov and pend is not None:
                    norm_mm(pend[0], pend[1], 0)
                    pend = None
                P.dma("sp", "fhst%d" % hs, HT[s][m, :, t0:t0 + 1024], hres[hs],
                      reads=["fhres%d" % hs], writes=[("HT", s, m)])
                if ov:
                    k = m
                    hb = k % 2
                    norm_load(k, ov_t0, hinB[hb], hinB_n[hb], "fhinB%d" % hb)
                    norm_sq_ut(k, hinB[hb], hinB_n[hb], k % 3, ov_gc)
                    pend = (k, k % 3)
            if ov:
                norm_mm(pend[0], pend[1], 0)
                norm_fin(0)

    def inproj_phase(s, l):
        pc = l * NPC_L
        P.barrier()
        A.reset()
        uT2 = [A.get([16, 1024], BF16) for _ in range(2)]
        nhin = [A.get([1024], F32) for _ in range(3)]
        nsq = [A.get([1024], BF16) for _ in range(3)]
        nrb = A.get([1024], F32)
        hn = ["inh%d" % i for i in range(3)]
        sqn = ["insq%d" % i for i in range(3)]
        cs_t = A.get([2, 1024], F32)
        sqb = [A.get([512], BF16) for _ in range(2)]
        lnv = [A.get([512], F32) for _ in range(2)]
        qn = [A.get([512], BF16) for _ in range(2)]
        t1 = [A.get([512], F32) for _ in range(2)]
        t2 = [A.get([512], F32) for _ in range(2)]
        qst = [A.get([1024], BF16) for _ in range(2)]
        xst = [A.get([1024], F32) for _ in range(2)]
        vst = A.get([8, 256], BF16)
        zst = [A.get([8, 256], BF16) for _ in range(2)]
        dst_ = A.get([8, 32], F32)
        sqb3 = sqb + [A.get([512], BF16)]
        gc0 = pc + 16

        def norm_steps(t0, uT, utag, b0):
            def mm(k):
                sl = k % 3
                P.add("pe", mmgroup([(banks[b0 + h][:, :], ones_b, nsq[sl][:, h * 512:(h + 1) * 512], k == 0, k == 15)
                                     for h in range(2)]), reads=[sqn[sl], "c:cbf"], writes=[BN[b0], BN[b0 + 1]])

            def p1(k):
                def f():
                    if k > 1:
                        mm(k - 2)
                    sl = k % 3
                    P.dma("sp", hn[sl], nhin[sl], HT[s][k, :, t0:t0 + 1024], reads=[("HT", s, k)], writes=[hn[sl]])
                    P.add("act", lambda e: e.activation(nsq[sl], nhin[sl], AF.Square), reads=[hn[sl]], writes=[sqn[sl]])
                return f

            def fin():
                mm(14)
                mm(15)
                for h in range(2):
                    P.add("act", (lambda h: lambda e: e.activation(nrb[:, h * 512:(h + 1) * 512], banks[b0 + h][:, :], AF.Ln,
                                                                    bias=EPS, scale=1.0 / D))(h),
                          reads=[BN[b0 + h]], writes=["inrb"])
                P.add("act", lambda e: e.activation(nrb, nrb, AF.Exp, scale=-0.5), reads=["inrb"], writes=["inrb"])

            def p2(k):
                def f():
                    sl = k % 3
                    P.dma("sp", hn[sl], nhin[sl], HT[s][k, :, t0:t0 + 1024], reads=[("HT", s, k)], writes=[hn[sl]])
                    P.add("dve", lambda e: e.scalar_tensor_tensor(uT[:, k, :], nhin[sl], pcol[:, gc0 + k:gc0 + k + 1],
                                                                  nrb, ALU.mult, ALU.mult),
                          reads=[hn[sl], "inrb", "c:pcol"], writes=[(utag, k)])
                return f
            return [p1(k) for k in range(16)] + [fin] + [p2(k) for k in range(16)]

        for st in norm_steps(0, uT2[0], "iuT0", 4):
            st()
        pending = []

        def pump():
            if pending:
                pending.pop(0)()

        for t in range(2):
            t0 = t * 1024
            uT = uT2[t]
            P.dma("sp", "irope", cs_t, rope_d[:, :, t0:t0 + 1024], writes=["irope"])
            uTr = [("iuT%d" % t, k) for k in range(16)]

            def load_w(name):
                off, ncols = WIN_OFF[name]
                sl = wslot_gu()
                P.dma("pool", "w:gu%d" % sl, wgu_sb[sl][:, 0:16 * ncols], win_d[l, :, off:off + 16 * ncols],
                      writes=["wgu%d" % sl])
                return sl, wgu_sb[sl][:, 0:16 * ncols].rearrange("p (k c) -> p k c", k=16)

            it = 0
            winfo = {}

            def stA(n):
                b_, half = divmod(n, 2)
                if half == 0:
                    name = ("q%d" % b_) if b_ < 8 else ("k%d" % (b_ - 8))
                    winfo[b_] = load_w(name)
                sl, wv = winfo[b_]
                ba = n % 3
                P.add("pe", mmgroup([(banks[ba][:, :], wv[:, k, :], uT[:, k, half * 512:(half + 1) * 512], k == 0, k == 15)
                                     for k in range(16)]), reads=["wgu%d" % sl] + uTr, writes=[BN[ba]])
                P.add("act", (lambda ba: lambda e: e.activation(sqb3[ba], banks[ba][:, :], AF.Square))(ba),
                      reads=[BN[ba]], writes=["isq%d" % ba])

            def stB(n):
                b_, half = divmod(n, 2)
                ba = n % 3
                r = n % 2
                bb = 3 + r
                gcol = pcol[:, pc + 56:pc + 57] if b_ < 8 else pcol[:, pc + 57:pc + 58]
                P.add("pe", mmgroup([(banks[bb][:, :], ones_b, sqb3[ba], True, True)]), reads=["isq%d" % ba, "c:cbf"],
                      writes=[BN[bb]])
                P.add("act", (lambda r, bb: lambda e: e.activation(lnv[r], banks[bb][:, :], AF.Ln, bias=EPS,
                                                                   scale=1.0 / 128))(r, bb),
                      reads=[BN[bb]], writes=["iln%d" % r])
                P.add("act", (lambda r: lambda e: e.activation(lnv[r], lnv[r], AF.Exp, scale=-0.5))(r),
                      reads=["iln%d" % r], writes=["iln%d" % r])
                P.add("dve", (lambda r, ba, gcol: lambda e: e.scalar_tensor_tensor(
                    qn[r], banks[ba][:, :], gcol, lnv[r], ALU.mult, ALU.mult))(r, ba, gcol),
                      reads=[BN[ba], "iln%d" % r, "c:pcol"], writes=["iqn%d" % r])

            def stC(n):
                b_, half = divmod(n, 2)
                r = n % 2
                bc = 5 + r
                qs = qst[b_ % 2]
                P.add("pe", mmgroup([(banks[bc][:, :], rot_b, qn[r], True, True)]), reads=["iqn%d" % r, "c:cbf"],
                      writes=[BN[bc]])
                P.add("dve", (lambda r, half: lambda e: e.tensor_tensor(
                    t1[r], qn[r], cs_t[:, 0, half * 512:(half + 1) * 512], ALU.mult))(r, half),
                      reads=["iqn%d" % r, "irope"], writes=["it1%d" % r])
                P.add("dve", (lambda r, half, bc: lambda e: e.tensor_tensor(
                    t2[r], banks[bc][:, :], cs_t[:, 1, half * 512:(half + 1) * 512], ALU.mult))(r, half, bc),
                      reads=[BN[bc], "irope"], writes=["it2%d" % r])
                P.add("dve", (lambda r, half, qs: lambda e: e.tensor_tensor(
                    qs[:, half * 512:(half + 1) * 512], t1[r], t2[r], ALU.add))(r, half, qs),
                      reads=["it1%d" % r, "it2%d" % r], writes=["iqst%d" % (b_ % 2)])
                if half == 1:
                    dst = QT[s][b_, :, t0:t0 + 1024] if b_ < 8 else KT[s][b_ - 8, :, t0:t0 + 1024]
                    P.dma("sp", "iqst%d" % (b_ % 2), dst, qs, reads=["iqst%d" % (b_ % 2)],
                          writes=[("QT", s, b_) if b_ < 8 else ("KT", s, b_ - 8)])

            NQ = 20
            for n in range(NQ + 2):
                if n < NQ:
                    stA(n)
                if 0 <= n - 1 < NQ:
                    stB(n - 1)
                if 0 <= n - 2 < NQ:
                    stC(n - 2)
            for b in range(12):
                sl, wv = load_w("x%d" % b)
                xs_ = xst[b % 2]
                for half in range(2):
                    r = it % 2
                    it += 1
                    ba = 6 + r
                    P.add("pe", mmgroup([(banks[ba][:, :], wv[:, k, :], uT[:, k, half * 512:(half + 1) * 512], k == 0, k == 15)
                                         for k in range(16)]), reads=["wgu%d" % sl] + uTr, writes=[BN[ba]])
                    if half == 0:
                        P.add("act", (lambda xs_, ba: lambda e: e.copy(xs_[:, 0:512], banks[ba][:, :]))(xs_, ba),
                              reads=[BN[ba]], writes=["ixst%d" % (b % 2)])
                    else:
                        P.add("dve", (lambda xs_, ba: lambda e: e.tensor_copy(xs_[:, 512:1024], banks[ba][:, :]))(xs_, ba),
                              reads=[BN[ba]], writes=["ixst%d" % (b % 2)])
                P.dma("sp", "ixst%d" % (b % 2), XBCT[s][b, :, t0:t0 + 1024], xs_, reads=["ixst%d" % (b % 2)],
                      writes=[("XBCT", s, b)])
            if t == 0:
                pending.extend(norm_steps(1024, uT2[1], "iuT1", 6))
            sl, wv = load_w("v")
            for c in range(8):
                ba = c % 2
                P.add("pe", mmgroup([(banks[ba][:, 0:256], uT[:, k, c * 128:(c + 1) * 128], wv[:, k, :], k == 0, k == 15)
                                     for k in range(16)]), reads=["wgu%d" % sl] + uTr, writes=[BN[ba]])
                P.add("dve", (lambda c, ba: lambda e: e.tensor_copy(vst[:, c, :], banks[ba][:, 0:256]))(c, ba),
                      reads=[BN[ba]], writes=[("ivst", c)])
                pump()
            P.dma("sp", "ivst", VV[s][t * 8:(t + 1) * 8].rearrange("c p f -> p c f"), vst,
                  reads=[("ivst", c) for c in range(8)], writes=[("V", s)])
            sl, wv = load_w("dt")
            for c in range(8):
                ba = 2 + c % 2
                P.add("pe", mmgroup([(banks[ba][:, 0:32], uT[:, k, c * 128:(c + 1) * 128], wv[:, k, :], k == 0, k == 15)
                                     for k in range(16)]), reads=["wgu%d" % sl] + uTr, writes=[BN[ba]])
                P.add("dve", (lambda c, ba: lambda e: e.tensor_copy(dst_[:, c, :], banks[ba][:, 0:32]))(c, ba),
                      reads=[BN[ba]], writes=[("idst", c)])
                pump()
            P.dma("sp", "idst", DTS[s][t * 8:(t + 1) * 8].rearrange("c p f -> p c f"), dst_,
                  reads=[("idst", c) for c in range(8)], writes=[("DT", s)])
            for zb in range(4):
                sl, wv = load_w("z%d" % zb)
                zs_ = zst[zb % 2]
                for c in range(8):
                    ba = 4 + c % 2
                    P.add("pe", mmgroup([(banks[ba][:, 0:256], uT[:, k, c * 128:(c + 1) * 128], wv[:, k, :], k == 0, k == 15)
                                         for k in range(16)]), reads=["wgu%d" % sl] + uTr, writes=[BN[ba]])
                    P.add("act", (lambda c, ba, zs_: lambda e: e.activation(zs_[:, c, :], banks[ba][:, 0:256], AF.Silu))(
                        c, ba, zs_), reads=[BN[ba]], writes=[("izst%d" % (zb % 2), c)])
                    pump()
                P.dma("sp", "izst%d" % (zb % 2),
                      ZZ[s][t * 8:(t + 1) * 8, :, zb * 256:(zb + 1) * 256].rearrange("c p f -> p c f"), zs_,
                      reads=[("izst%d" % (zb % 2), c) for c in range(8)], writes=[("Z", s, zb)])
            while pending:
                pump()

    def ssd_phase(s, l):
        pc = l * NPC_L
        pr = l * NPR_L
        P.barrier()
        A.reset()
        rowp = A.get([1104], F32)
        P.dma("sp", "srowp", rowp, prow_d[:, pr:pr + 1104].partition_broadcast(128), writes=["srowp"])
        gain = rowp[:, 0:1024]
        dtb = rowp[:, 1024:1056]
        alog = rowp[:, 1056:1088]
        dsk = rowp[:, 1088:1104]
        xs_tok = A.get([16, 1024], BF16)
        btok = A.get([16, 256], BF16)
        bcT = A.get([4, L], BF16)
        dtr = A.get([16, 32], F32)
        dt_ = A.get([16, 32], F32)
        dtA = A.get([16, 32], F32)
        cs = A.get([16, 32], F32)
        negcs = A.get([16, 32], F32)
        atot = A.get([16, 32], F32)
        ecs = A.get([16, 32], F32)
        dtd = A.get([16, 32], F32)
        eat = A.get([16, 32], F32)
        negA = A.get([32], F32)
        mk = A.mark()
        mx = A.get([16, 32], F32)
        nab = A.get([16, 32], F32)
        csTf = A.get([L], F32)
        csTb = A.get([L], F32)
        xin = [A.get([L + 4], F32) for _ in range(2)]
        acc = [A.get([L], F32) for _ in range(2)]
        xo = [A.get([L], BF16) for _ in range(2)]
        for i in range(2):
            P.add("dve", (lambda i: lambda e: e.memset(xin[i][:, 0:2], 0.0))(i), writes=["sxin%d" % i])
            P.add("dve", (lambda i: lambda e: e.memset(xin[i][:, L + 2:L + 4], 0.0))(i), writes=["sxin%d" % i])
        def conv_s1(b):
            sl = b % 2
            P.dma("sp", "sxin%d" % sl, xin[sl][:, 2:L + 2], XBCT[s][b], reads=[("XBCT", s, b)], writes=["sxin%d" % sl])
            wc = pc + 58 + b * 5
            P.add("act", (lambda sl, wc: lambda e: e.activation(acc[sl], xin[sl][:, 0:L], AF.Copy,
                                                                 scale=pcol[:, wc:wc + 1]))(sl, wc),
                  reads=["sxin%d" % sl, "c:pcol"], writes=["sacc%d" % sl])

        def conv_s2(b):
            sl = b % 2
            wc = pc + 58 + b * 5
            for k in range(1, 5):
                P.add("dve", (lambda sl, wc, k: lambda e: e.scalar_tensor_tensor(
                    acc[sl], xin[sl][:, k:k + L], pcol[:, wc + k:wc + k + 1], acc[sl], ALU.mult, ALU.add))(sl, wc, k),
                      reads=["sxin%d" % sl, "sacc%d" % sl], writes=["sacc%d" % sl])

        def conv_s3(b):
            sl = b % 2
            if b < 10:
                dst = xo[sl]
                dname = "sxo%d" % sl
            else:
                dst = bcT[:, b - 8, :]
                dname = ("sbcT", b - 8)
            P.add("act", (lambda sl, dst, b: lambda e: e.activation(dst, acc[sl], AF.Silu,
                                                                    bias=pcol[:, pc + 118 + b:pc + 119 + b]))(sl, dst, b),
                  reads=["sacc%d" % sl, "c:pcol"], writes=[dname])
            if b in (8, 9):
                P.add("dve", (lambda sl, b: lambda e: e.tensor_copy(bcT[:, b - 8, :], xo[sl]))(sl, b),
                      reads=[dname], writes=[("sbcT", b - 8)])
            if b < 10:
                for half in range(2):
                    bi = half
                    bv = bf_bank(bi, [8, 128])
                    P.add("pe", trgroup([(bv[:, i, :], xo[sl][:, (half * 8 + i) * 128:(half * 8 + i + 1) * 128], ident_b)
                                         for i in range(8)]), reads=[dname, "c:cbf"], writes=[BN[bi]])
                    if b < 8:
                        dst2 = xs_tok[:, half * 8:(half + 1) * 8, b * 128:(b + 1) * 128]
                        rname = ("sxs", b, half)
                    else:
                        dst2 = btok[:, half * 8:(half + 1) * 8, (b - 8) * 128:(b - 7) * 128]
                        rname = ("sbtok", b, half)
                    if half == 0:
                        P.add("act", (lambda dst2, bv: lambda e: e.copy(dst2, bv))(dst2, bv), reads=[BN[bi]], writes=[rname])
                    else:
                        P.add("dve", (lambda dst2, bv: lambda e: e.tensor_copy(dst2, bv))(dst2, bv), reads=[BN[bi]],
                              writes=[rname])

        conv_s1(0)
        for b in range(12):
            if b + 1 < 12:
                conv_s1(b + 1)
            conv_s2(b)
            conv_s3(b)
        xs_r = [("sxs", b, h) for b in range(8) for h in range(2)]
        btok_r = [("sbtok", b, h) for b in (8, 9) for h in range(2)]
        P.dma("sp", "sdtr", dtr, DTS[s].rearrange("c p f -> p c f"), reads=[("DT", s)], writes=["sdtr"])
        bc3 = lambda ap: ap.unsqueeze(1).broadcast_to([128, 16, 32])
        P.add("dve", lambda e: e.tensor_tensor(dtr, dtr, bc3(dtb), ALU.add), reads=["sdtr", "srowp"], writes=["sdtr"])
        P.add("dve", lambda e: e.tensor_scalar_max(mx, dtr, 0.0), reads=["sdtr"], writes=["smx"])
        P.add("dve", lambda e: e.scalar_tensor_tensor(nab, mx, -2.0, dtr, ALU.mult, ALU.add), reads=["smx", "sdtr"],
              writes=["snab"])
        P.add("act", lambda e: e.activation(nab, nab, AF.Exp), reads=["snab"], writes=["snab"])
        P.add("act", lambda e: e.activation(nab, nab, AF.Ln, bias=1.0), reads=["snab"], writes=["snab"])
        P.add("dve", lambda e: e.tensor_tensor(dt_, mx, nab, ALU.add), reads=["smx", "snab"], writes=["sdt"])
        P.add("act", lambda e: e.activation(negA, alog, AF.Exp), reads=["srowp"], writes=["snegA"])
        P.add("dve", lambda e: e.tensor_scalar(negA, negA, -1.0, 0.0, ALU.mult, ALU.add), reads=["snegA"], writes=["snegA"])
        P.add("dve", lambda e: e.tensor_tensor(dtA, dt_, bc3(negA), ALU.mult), reads=["sdt", "snegA"], writes=["sdtA"])
        csb = banks[2][:, :].rearrange("p (c f) -> p c f", c=16)
        items = []
        for c in range(16):
            items.append((csb[:, c, 0:16], triu_f, dtA[:, c, 0:16], True, True))
            items.append((csb[:, c, 16:32], tril_f, dtA[:, c, 16:32], True, True))
        P.add("pe", mmgroup(items), reads=["sdtA", "c:cmat"], writes=[BN[2]])
        P.add("pe", mmgroup([(banks[3][:, :], ones_f, dtA.rearrange("p c f -> p (c f)"), True, True)]),
              reads=["sdtA", "c:cmat"], writes=[BN[3]])
        P.add("dve", lambda e: e.tensor_copy(cs, csb), reads=[BN[2]], writes=["scs"])
        P.add("dve", lambda e: e.tensor_scalar(negcs, cs, -1.0, 0.0, ALU.mult, ALU.add), reads=["scs"], writes=["snegcs"])
        P.add("dve", lambda e: e.tensor_copy(atot.rearrange("p c f -> p (c f)"), banks[3][:, :]), reads=[BN[3]],
              writes=["satot"])
        P.add("act", lambda e: e.activation(ecs, cs, AF.Exp), reads=["scs"], writes=["secs"])
        P.add("dve", lambda e: e.tensor_tensor(dtd, atot, cs, ALU.subtract), reads=["satot", "scs"], writes=["sdtd"])
        P.add("act", lambda e: e.activation(dtd, dtd, AF.Exp), reads=["sdtd"], writes=["sdtd"])
        P.add("dve", lambda e: e.tensor_tensor(dtd, dtd, dt_, ALU.mult), reads=["sdtd", "sdt"], writes=["sdtd"])
        P.add("act", lambda e: e.activation(eat, atot, AF.Exp), reads=["satot"], writes=["seat"])
        for c in range(16):
            P.add("pe", mmgroup([
                (banks[c // 4][0:32, (c % 4) * 128:(c % 4 + 1) * 128], dtA[:, c, :], triu_f, True, True),
                (banks[4 + c // 4][0:32, (c % 4) * 128:(c % 4 + 1) * 128], dtA[:, c, :], tril_f, True, True)]),
                  reads=["sdtA", "c:cmat"], writes=[BN[c // 4], BN[4 + c // 4]])
        for q4 in range(4):
            P.add("act", (lambda q4: lambda e: e.copy(csTf[0:32, q4 * 512:(q4 + 1) * 512], banks[q4][0:32, :]))(q4),
                  reads=[BN[q4]], writes=[("scsTf", q4)])
            P.add("dve", (lambda q4: lambda e: e.tensor_copy(csTb[0:32, q4 * 512:(q4 + 1) * 512], banks[4 + q4][0:32, :]))(q4),
                  reads=[BN[4 + q4]], writes=[("scsTb", q4)])
        P.dma("sp", "scsTf", CST[s][:, 0:16, :].rearrange("c h l -> h c l"),
              csTf[0:16, :].rearrange("h (c l) -> h c l", c=16),
              reads=[("scsTf", q4) for q4 in range(4)], writes=[("CST", s, 0)])
        P.dma("sp", "scsTb", CST[s][:, 16:32, :].rearrange("c h l -> h c l"),
              csTb[16:32, :].rearrange("h (c l) -> h c l", c=16),
              reads=[("scsTb", q4) for q4 in range(4)], writes=[("CST", s, 1)])
        P.barrier()
        A.release(mk)
        sinb = A.get([16, 512], BF16)
        zsl = [A.get([512], BF16) for _ in range(2)]
        sTs = [A.get([4, 128], BF16) for _ in range(2)]
        Sb = A.get([512], F32)
        Sf = A.get([512], F32)
        Sfb = A.get([512], BF16)
        Xd = [A.get([512], BF16) for _ in range(3)]
        Xf = [A.get([512], BF16) for _ in range(3)]
        Xb = [A.get([512], BF16) for _ in range(3)]
        crb = [A.get([2, 8, 128], F32) for _ in range(2)]
        Gt = [A.get([2, 8, 128], BF16) for _ in range(2)]
        cbtm = [A.get([2, 128], F32) for _ in range(2)]
        ta = [A.get([512], F32) for _ in range(2)]
        tb = [A.get([512], F32) for _ in range(2)]
        yo = [A.get([512], BF16) for _ in range(2)]
        td = [A.get([512], BF16) for _ in range(3)]
        ssq = A.get([8], F32)
        mask01 = cmat[:, 128:384].rearrange("p (a b) -> p a b", a=2)
        h3 = lambda ap: ap.rearrange("p (h q) -> p h q", h=8)
        hb3 = lambda ap: ap.unsqueeze(2).broadcast_to([128, 8, 64])
        cst_r = [("CST", s, 0), ("CST", s, 1)]
        for g in range(2):
            P.add("dve", lambda e: e.memset(Sb, 0.0), writes=["sSb"])
            P.add("dve", lambda e: e.memset(Sf, 0.0), writes=["sSf"])
            P.add("dve", lambda e: e.memset(Sfb, 0.0), writes=["sSfb"])
            for c in range(15, -1, -1):
                r = c % 2
                xsl = xs_tok[:, c, g * 512:(g + 1) * 512]
                hbo = 16 + g * 8
                P.add("pool", (lambda r, xsl, c, hbo: lambda e: e.tensor_tensor(
                    h3(Xd[r]), h3(xsl), hb3(dtd[:, c, hbo:hbo + 8]), ALU.mult))(r, xsl, c, hbo),
                      reads=xs_r + ["sdtd"], writes=["sXd%d" % r])
                sbk = 6 if c % 2 else 3
                P.add("pe", mmgroup([(banks[sbk][:, :], btok[:, c, g * 128:(g + 1) * 128], Xd[r], True, True)]),
                      reads=btok_r + ["sXd%d" % r], writes=[BN[sbk]])
                P.add("act", (lambda c: lambda e: e.copy(sinb[:, c, :], Sb))(c), reads=["sSb"], writes=[("ssinb", c)])
                P.add("dve", (lambda c, hbo: lambda e: e.tensor_tensor(h3(Sb), h3(Sb), hb3(eat[:, c, hbo:hbo + 8]),
                                                                        ALU.mult))(c, hbo),
                      reads=["sSb", "seat"], writes=["sSb"])
                P.add("dve", (lambda sbk: lambda e: e.tensor_tensor(Sb, Sb, banks[sbk][:, :], ALU.add))(sbk),
                      reads=["sSb", BN[sbk]], writes=["sSb"])

            def load_cr(c):
                r = c % 2
                hfo = g * 8
                hbo = 16 + g * 8
                P.dma("sp", "scrbf%d" % r, crb[r][:, 0, :, :].rearrange("p h l -> p (h l)"),
                      CST[s][c:c + 1, hfo:hfo + 8, :].rearrange("c h l -> c (h l)").partition_broadcast(128),
                      reads=cst_r, writes=[("scrb%d" % r, 0)] + [("scrbL%d" % r, 0, h) for h in range(8)])
                P.dma("sp", "scrbb%d" % r, crb[r][:, 1, :, :].rearrange("p h l -> p (h l)"),
                      CST[s][c:c + 1, hbo:hbo + 8, :].rearrange("c h l -> c (h l)").partition_broadcast(128),
                      reads=cst_r, writes=[("scrb%d" % r, 1)] + [("scrbL%d" % r, 1, h) for h in range(8)])

            def load_z(c):
                r = c % 2
                P.dma("sp", "szsl%d" % r, zsl[r], ZZ[s][c, :, g * 512:(g + 1) * 512],
                      reads=[("Z", s, zb) for zb in range(4)], writes=["szsl%d" % r])

            load_cr(0)
            load_z(0)
            load_cr(1)
            load_z(1)

            def front_a(c):
                r = c % 2
                x3 = c % 3
                xsl = xs_tok[:, c, g * 512:(g + 1) * 512]
                hfo = g * 8
                hbo = 16 + g * 8
                csl = slice(c * 128, (c + 1) * 128)
                P.add("pe", mmgroup([(banks[0][:, 0:128], bcT[:, g, csl], bcT[:, 2 + g, csl], True, True)]),
                      reads=[("sbcT", g), ("sbcT", 2 + g)], writes=[BN[0]])
                P.add("pool", (lambda x3, xsl, c, hfo: lambda e: e.tensor_tensor(
                    h3(Xf[x3]), h3(xsl), hb3(dt_[:, c, hfo:hfo + 8]), ALU.mult))(x3, xsl, c, hfo),
                      reads=xs_r + ["sdt"], writes=["sXf%d" % x3])
                P.add("pool", (lambda x3, xsl, c, hbo: lambda e: e.tensor_tensor(
                    h3(Xb[x3]), h3(xsl), hb3(dt_[:, c, hbo:hbo + 8]), ALU.mult))(x3, xsl, c, hbo),
                      reads=xs_r + ["sdt"], writes=["sXb%d" % x3])
                P.add("pool", (lambda x3, xsl, c, hfo: lambda e: e.tensor_tensor(
                    h3(Xd[x3]), h3(xsl), hb3(dtd[:, c, hfo:hfo + 8]), ALU.mult))(x3, xsl, c, hfo),
                      reads=xs_r + ["sdtd"], writes=["sXd%d" % x3])
                P.add("pool", (lambda x3, xsl, hfo: lambda e: e.tensor_tensor(
                    h3(td[x3]), h3(xsl), hb3(dsk[:, hfo:hfo + 8]), ALU.mult))(x3, xsl, hfo),
                      reads=xs_r + ["srowp"], writes=["std%d" % x3])
                P.add("dve", (lambda r: lambda e: e.tensor_tensor(
                    cbtm[r], mask01, banks[0][:, 0:128].unsqueeze(1).broadcast_to([128, 2, 128]), ALU.mult))(r),
                      reads=[BN[0], "c:cmat"], writes=["scbtm%d" % r])
                for d in range(2):
                    for h in range(8):
                        hd = (hfo if d == 0 else hbo) + h
                        P.add("act", (lambda r, d, h, c, hd: lambda e: e.activation(
                            crb[r][:, d, h, :], crb[r][:, d, h, :], AF.Exp, bias=negcs[:, c, hd:hd + 1]))(r, d, h, c, hd),
                              reads=[("scrb%d" % r, d), "snegcs"], writes=[("scrbL%d" % r, d, h)])

            def front_b(c):
                r = c % 2
                for d in range(2):
                    P.add("dve", (lambda r, d: lambda e: e.scalar_tensor_tensor(
                        Gt[r][:, d, :, :], crb[r][:, d, :, :], 1.0,
                        cbtm[r][:, d, :].unsqueeze(1).broadcast_to([128, 8, 128]), ALU.min, ALU.mult))(r, d),
                          reads=[("scrbL%d" % r, d, h) for h in range(8)] + ["scbtm%d" % r],
                          writes=[("sGt%d" % r, d)])

            def back_a(c):
                r = c % 2
                x3 = c % 3
                hfo = g * 8
                hbo = 16 + g * 8
                csl = slice(c * 128, (c + 1) * 128)
                yb = 1 + r
                items = []
                for h in range(8):
                    items.append((banks[yb][:, h * 64:(h + 1) * 64], Gt[r][:, 0, h, :], Xf[x3][:, h * 64:(h + 1) * 64], True, False))
                    items.append((banks[yb][:, h * 64:(h + 1) * 64], Gt[r][:, 1, h, :], Xb[x3][:, h * 64:(h + 1) * 64], False, True))
                P.add("pe", mmgroup(items), reads=[("sGt%d" % r, 0), ("sGt%d" % r, 1), "sXf%d" % x3, "sXb%d" % x3],
                      writes=[BN[yb]])
                P.add("pe", mmgroup([(banks[4][:, :], bcT[:, 2 + g, csl], Sfb, True, True)]),
                      reads=[("sbcT", 2 + g), "sSfb"], writes=[BN[4]])
                P.add("pe", mmgroup([(banks[5][:, :], bcT[:, 2 + g, csl], sinb[:, c, :], True, True)]),
                      reads=[("sbcT", 2 + g), ("ssinb", c)], writes=[BN[5]])
                P.add("dve", (lambda r, c, hfo: lambda e: e.tensor_tensor(
                    h3(ta[r]), h3(banks[4][:, :]), hb3(ecs[:, c, hfo:hfo + 8]), ALU.mult))(r, c, hfo),
                      reads=[BN[4], "secs"], writes=["sta%d" % r])
                P.add("dve", (lambda r, c, hbo: lambda e: e.tensor_tensor(
                    h3(tb[r]), h3(banks[5][:, :]), hb3(ecs[:, c, hbo:hbo + 8]), ALU.mult))(r, c, hbo),
                      reads=[BN[5], "secs"], writes=["stb%d" % r])
                P.add("dve", (lambda r, yb: lambda e: e.tensor_tensor(ta[r], ta[r], banks[yb][:, :], ALU.add))(r, yb),
                      reads=["sta%d" % r, BN[yb]], writes=["sta%d" % r])
                P.add("dve", (lambda r: lambda e: e.tensor_tensor(ta[r], ta[r], tb[r], ALU.add))(r),
                      reads=["sta%d" % r, "stb%d" % r], writes=["sta%d" % r])
                P.add("dve", (lambda r, x3: lambda e: e.tensor_tensor(ta[r], ta[r], td[x3], ALU.add))(r, x3),
                      reads=["sta%d" % r, "std%d" % x3], writes=["sta%d" % r])
                P.add("dve", (lambda r: lambda e: e.tensor_tensor(ta[r], ta[r], zsl[r], ALU.mult))(r),
                      reads=["sta%d" % r, "szsl%d" % r], writes=["sta%d" % r])
                P.add("dve", (lambda r: lambda e: e.memset(ssq[:, r * 4:r * 4 + 1], 0.0))(r), writes=["sssq%d" % r])
                P.add("act", (lambda r: lambda e: e.activation(yo[r], ta[r], AF.Square, accum_out=ssq[:, r * 4:r * 4 + 1]))(r),
                      reads=["sta%d" % r, "sssq%d" % r], writes=["sssq%d" % r, "syo%d" % r])
                P.add("act", (lambda r: lambda e: e.activation(ssq[:, r * 4 + 1:r * 4 + 2], ssq[:, r * 4:r * 4 + 1], AF.Ln,
                                                               bias=EPS, scale=1.0 / 512))(r),
                      reads=["sssq%d" % r], writes=["sssq%d" % r])
                P.add("act", (lambda r: lambda e: e.activation(ssq[:, r * 4 + 2:r * 4 + 3], ssq[:, r * 4 + 1:r * 4 + 2], AF.Exp,
                                                               scale=-0.5))(r),
                      reads=["sssq%d" % r], writes=["sssq%d" % r])

            def back_b(c):
                r = c % 2
                x3 = c % 3
                hfo = g * 8
                P.add("dve", (lambda r, g: lambda e: e.scalar_tensor_tensor(
                    yo[r], ta[r], ssq[:, r * 4 + 2:r * 4 + 3], gain[:, g * 512:(g + 1) * 512], ALU.mult, ALU.mult))(r, g),
                      reads=["sta%d" % r, "sssq%d" % r, "srowp"], writes=["syo%d" % r])
                bv = bf_bank(7, [4, 128])
                P.add("pe", trgroup([(bv[:, i, :], yo[r][:, i * 128:(i + 1) * 128], ident_b) for i in range(4)]),
                      reads=["syo%d" % r, "c:cbf"], writes=[BN[7]])
                P.add("act", (lambda r, bv: lambda e: e.copy(sTs[r], bv))(r, bv),
                      reads=[BN[7]], writes=["ssT%d" % r])
                P.dma("sp", "ssT%d" % r, MIXT[s][8 + g * 4:12 + g * 4, :, c * 128:(c + 1) * 128].rearrange("k p t -> p k t"),
                      sTs[r], reads=["ssT%d" % r], writes=[("MIXT", s, "s", g, c)])
                P.add("pe", mmgroup([(banks[6][:, :], btok[:, c, g * 128:(g + 1) * 128], Xd[x3], True, True)]),
                      reads=btok_r + ["sXd%d" % x3], writes=[BN[6]])
                P.add("dve", (lambda c, hfo: lambda e: e.tensor_tensor(h3(Sf), h3(Sf), hb3(eat[:, c, hfo:hfo + 8]),
                                                                        ALU.mult))(c, hfo),
                      reads=["sSf", "seat"], writes=["sSf"])
                P.add("dve", lambda e: e.tensor_tensor(Sf, Sf, banks[6][:, :], ALU.add), reads=["sSf", BN[6]],
                      writes=["sSf"])
                P.add("act", lambda e: e.copy(Sfb, Sf), reads=["sSf"], writes=["sSfb"])

            front_a(0)
            front_b(0)
            load_cr(2)
            front_a(1)
            front_b(1)
            for c in range(16):
                if c + 3 < 16:
                    load_cr(c + 3)
                if c + 2 < 16:
                    front_a(c + 2)
                back_a(c)
                if c + 2 < 16:
                    front_b(c + 2)
                back_b(c)
                if c + 2 < 16:
                    load_z(c + 2)

    def attn_phase(s, l):
        pc = l * NPC_L
        pr = l * NPR_L
        P.barrier()
        A.reset()
        kT = A.get([2, L], BF16)
        vv = A.get([16, 256], BF16)
        qT = A.get([8, L], BF16)
        oT = A.get([8, L], F32)
        qk = A.get([256], F32)
        negB = A.get([4], F32)
        pt = [A.get([1024], BF16) for _ in range(3)]
        rden = [A.get([512], F32) for _ in range(2)]
        P.dma("sp", "akT", kT, KT[s].rearrange("k p t -> p k t"), reads=[("KT", s, 0), ("KT", s, 1)], writes=["akT"])
        P.dma("sp", "avv", vv, VV[s].rearrange("c p f -> p c f"), reads=[("V", s)], writes=["avv"])
        P.dma("sp", "aqT", qT, QT[s].rearrange("k p t -> p k t"), reads=[("QT", s, b) for b in range(8)], writes=["aqT"])
        P.dma("sp", "aqk", qk, prow_d[:, pr + 1104:pr + 1360].partition_broadcast(128), writes=["aqk"])
        P.add("dve", lambda e: e.tensor_reduce(negB[:, 0:1], qk[:, 0:128], AX.X, ALU.max, apply_absolute_value=True),
              reads=["aqk"], writes=["anegB"])
        P.add("dve", lambda e: e.tensor_reduce(negB[:, 1:2], qk[:, 128:256], AX.X, ALU.max, apply_absolute_value=True),
              reads=["aqk", "anegB"], writes=["anegB"])
        scale = 128.0 ** -0.5
        P.add("dve", lambda e: e.scalar_tensor_tensor(negB[:, 2:3], negB[:, 0:1], -scale * 128.0, negB[:, 1:2],
                                                      ALU.mult, ALU.mult), reads=["anegB"], writes=["anegB"])
        items = [(kv, qg, kv * 4 + h4) for kv in range(2) for qg in range(4) for h4 in range(4)]
        NS = len(items) * 8
        LOOK = 1

        def emit_S(n):
            it, s2 = divmod(n, 8)
            kv, qg, hd = items[it]
            sb_ = (n % 2) * 2
            pr_ = n % 3
            qsl = qT[:, hd, qg * 512:(qg + 1) * 512]
            P.add("pe", mmgroup([(banks[sb_ + j][:, :], kT[:, kv, (s2 * 2 + j) * 128:(s2 * 2 + j + 1) * 128], qsl, True, True)
                                 for j in range(2)]),
                  reads=["akT", "aqT"], writes=[BN[sb_], BN[sb_ + 1]])
            P.add("act", (lambda pr_, sb_: lambda e: e.activation(pt[pr_], big[sb_ // 2][:, :],
                                                                   AF.Exp, bias=negB[:, 2:3], scale=scale))(pr_, sb_),
                  reads=[BN[sb_], BN[sb_ + 1], "anegB"], writes=[("apt%d" % pr_, 0), ("apt%d" % pr_, 1)])

        def emit_PV(n):
            it, s2 = divmod(n, 8)
            kv, qg, hd = items[it]
            pr_ = n % 3
            ob = 4 + it % 2
            db = 6 + it % 2
            its = []
            for j in range(2):
                sc = s2 * 2 + j
                its.append((banks[ob][:, :], vv[:, sc, kv * 128:(kv + 1) * 128], pt[pr_][:, j * 512:(j + 1) * 512], sc == 0, sc == 15))
                its.append((banks[db][:, :], ones_b, pt[pr_][:, j * 512:(j + 1) * 512], sc == 0, sc == 15))
            P.add("pe", mmgroup(its), reads=["avv", ("apt%d" % pr_, 0), ("apt%d" % pr_, 1), "c:cbf"], writes=[BN[ob], BN[db]])
            if s2 == 7:
                rr = it % 2
                P.add("dve", (lambda rr, db: lambda e: e.reciprocal(rden[rr], banks[db][:, :]))(rr, db),
                      reads=[BN[db]], writes=["arden%d" % rr])
                P.add("dve", (lambda rr, ob, hd, qg: lambda e: e.tensor_tensor(
                    oT[:, hd, qg * 512:(qg + 1) * 512], banks[ob][:, :], rden[rr], ALU.mult))(rr, ob, hd, qg),
                      reads=[BN[ob], "arden%d" % rr], writes=[("aoT", hd, qg)])

        for n in range(NS + LOOK):
            if n < NS:
                emit_S(n)
            if n >= LOOK:
                emit_PV(n - LOOK)
        sq = [A.get([512], BF16) for _ in range(2)]
        rb = A.get([L], F32)
        mst = [A.get([L], BF16) for _ in range(2)]
        for qg in range(4):
            bi = 7
            for hd in range(8):
                r = hd % 2
                P.add("act", (lambda r, hd, qg: lambda e: e.activation(sq[r], oT[:, hd, qg * 512:(qg + 1) * 512],
                                                                        AF.Square))(r, hd, qg),
                      reads=[("aoT", hd, qg)], writes=["asq%d" % r])
                P.add("pe", mmgroup([(banks[bi][:, :], ones_b, sq[r], hd == 0, hd == 7)]), reads=["asq%d" % r, "c:cbf"],
                      writes=[BN[bi]])
            P.add("act", (lambda qg, bi: lambda e: e.activation(rb[:, qg * 512:(qg + 1) * 512], banks[bi][:, :], AF.Ln,
                                                                bias=EPS, scale=1.0 / 1024))(qg, bi),
                  reads=[BN[bi]], writes=[("arb", qg)])
            P.add("act", (lambda qg: lambda e: e.activation(rb[:, qg * 512:(qg + 1) * 512], rb[:, qg * 512:(qg + 1) * 512],
                                                            AF.Exp, scale=-0.5))(qg),
                  reads=[("arb", qg)], writes=[("arb", qg)])
        for hd in range(8):
            r = hd % 2
            P.add("dve", (lambda r, hd: lambda e: e.scalar_tensor_tensor(
                mst[r], oT[:, hd, :], pcol[:, pc + 48 + hd:pc + 49 + hd], rb, ALU.mult, ALU.mult))(r, hd),
                  reads=[("aoT", hd, qg) for qg in range(4)] + [("arb", qg) for qg in range(4)] + ["c:pcol"],
                  writes=["amst%d" % r])
            P.dma("sp", "amst%d" % r, MIXT[s][hd], mst[r], reads=["amst%d" % r], writes=[("MIXT", s, hd)])

    def outproj_phase(s, l):
        P.barrier()
        A.reset()
        mT2 = [A.get([16, 1024], BF16) for _ in range(2)]
        hres = [A.get([1024], F32) for _ in range(3)]
        for t in range(2):
            P.dma("sp", "omT%d" % t, mT2[t], MIXT[s][:, :, t * 1024:(t + 1) * 1024].rearrange("k p t -> p k t"),
                  reads=[("MIXT", s, k) for k in range(8)] + [("MIXT", s, "s", g, c) for g in range(2) for c in range(16)],
                  writes=["omT%d" % t])
        for t in range(2):
            t0 = t * 1024
            mT = mT2[t]
            for m in range(16):
                sl = wslot_gu()
                P.dma("pool", "w:gu%d" % sl, wgu_sb[sl][:, 0:2048], wout_d[l, m], writes=["wgu%d" % sl])
                wv = wgu_sb[sl][:, 0:2048].rearrange("p (k c) -> p k c", k=16)
                hs = m % 3
                P.dma("sp", "ohres%d" % hs, hres[hs], HT[s][m, :, t0:t0 + 1024], reads=[("HT", s, m)],
                      writes=["ohres%d" % hs])
                for half in range(2):
                    bi = (m % 2) * 2 + half
                    P.add("pe", mmgroup([(banks[bi][:, :], wv[:, k, :], mT[:, k, half * 512:(half + 1) * 512], k == 0, k == 15)
                                         for k in range(16)]), reads=["wgu%d" % sl, "omT%d" % t], writes=[BN[bi]])
                    P.add("dve", (lambda hs, half, bi: lambda e: e.tensor_tensor(
                        hres[hs][:, half * 512:(half + 1) * 512], banks[bi][:, :],
                        hres[hs][:, half * 512:(half + 1) * 512], ALU.add))(hs, half, bi),
                          reads=[BN[bi], "ohres%d" % hs], writes=["ohres%d" % hs])
                P.dma("sp", "ohst%d" % hs, HT[s][m, :, t0:t0 + 1024], hres[hs], reads=["ohres%d" % hs],
                      writes=[("HT", s, m)])

    def final_phase(s):
        gc0 = depth * NPC_L
        P.barrier()
        A.reset()
        hh2 = [A.get([16, 512], F32) for _ in range(2)]
        sq = [A.get([512], BF16) for _ in range(2)]
        rb2 = [A.get([512], F32) for _ in range(2)]
        ost = [A.get([D], F32) for _ in range(2)]

        def load(tg):
            hs = tg % 2
            P.dma("sp", "zhh%d" % hs, hh2[hs], HT[s][:, :, tg * 512:(tg + 1) * 512].rearrange("k p t -> p k t"),
                  reads=[("HT", s, k) for k in range(16)],
                  writes=["zhh%d" % hs] + [("zhn%d" % hs, k) for k in range(16)])

        load(0)
        load(1)
        for tg in range(4):
            hs = tg % 2
            hh = hh2[hs]
            rb = rb2[hs]
            sb_ = 6 + hs
            for k in range(16):
                r = k % 2
                P.add("act", (lambda r, k, hh: lambda e: e.activation(sq[r], hh[:, k, :], AF.Square))(r, k, hh),
                      reads=["zhh%d" % hs], writes=["zsq%d" % r])
                P.add("pe", mmgroup([(banks[sb_][:, :], ones_b, sq[r], k == 0, k == 15)]), reads=["zsq%d" % r, "c:cbf"],
                      writes=[BN[sb_]])
            P.add("act", (lambda rb, sb_: lambda e: e.activation(rb, banks[sb_][:, :], AF.Ln, bias=EPS, scale=1.0 / D))(rb, sb_),
                  reads=[BN[sb_]], writes=["zrb%d" % hs])
            P.add("act", (lambda rb: lambda e: e.activation(rb, rb, AF.Exp, scale=-0.5))(rb), reads=["zrb%d" % hs],
                  writes=["zrb%d" % hs])
            for k in range(16):
                P.add("dve", (lambda k, hh, rb: lambda e: e.scalar_tensor_tensor(hh[:, k, :], hh[:, k, :],
                                                                                 pcol[:, gc0 + k:gc0 + k + 1], rb, ALU.mult,
                                                                                 ALU.mult))(k, hh, rb),
                      reads=["zhh%d" % hs, "zrb%d" % hs, "c:pcol"], writes=[("zhn%d" % hs, k)])
            for c4 in range(4):
                o = ost[c4 % 2]
                for k4 in range(4):
                    bi = (c4 * 4 + k4) % 4
                    bv = banks[bi][:, :].rearrange("p (a b) -> p a b", a=4)
                    P.add("pe", trgroup([(bv[:, i, :], hh[:, k4 * 4 + i, c4 * 128:(c4 + 1) * 128], ident_f) for i in range(4)]),
                          reads=[("zhn%d" % hs, k4 * 4 + i) for i in range(4)] + ["c:cmat"], writes=[BN[bi]])
                    dst = o[:, k4 * 512:(k4 + 1) * 512]
                    if k4 % 2 == 0:
                        P.add("dve", (lambda dst, bi: lambda e: e.tensor_copy(dst, banks[bi][:, :]))(dst, bi),
                              reads=[BN[bi]], writes=[("zost%d" % (c4 % 2), k4)])
                    else:
                        P.add("act", (lambda dst, bi: lambda e: e.copy(dst, banks[bi][:, :]))(dst, bi),
                              reads=[BN[bi]], writes=[("zost%d" % (c4 % 2), k4)])
                c = tg * 4 + c4
                op = P.dma("sp", "zost%d" % (c4 % 2), out_d[s, c * 128:(c + 1) * 128, :], o,
                           reads=[("zost%d" % (c4 % 2), k4) for k4 in range(4)], writes=[("OUT", s, c)])
                final_ops.append(op)
            if tg + 2 < 4:
                load(tg + 2)

    done = False
    for s in range(nseq):
        if done:
            break
        init_phase(s)
        if stop_after == "init":
            done = True
            break
        for l in range(depth):
            steps = [("ffn1", lambda: ffn_phase(s, l, 0, chain_prev=(l > 0 and stop_after is None))), ("inproj", lambda: inproj_phase(s, l)),
                     ("ssd", lambda: ssd_phase(s, l)), ("attn", lambda: attn_phase(s, l)),
                     ("outproj", lambda: outproj_phase(s, l)), ("ffn2", lambda: ffn_phase(s, l, 1, next_gc=((l + 1) * NPC_L if (l + 1 < depth and stop_after is None) else None)))]
            for name, fn in steps:
                fn()
                if stop_after == name and l == 0:
                    done = True
                    break
            if done:
                break
        if not done:
            final_phase(s)
    if done:
        final_ops.extend([o for k, o in P.dma_last.items() if not str(k).startswith("w:") and not str(k).startswith("c")])
    P.emit(final_wait_ops=final_ops)
    es.close()
    return nc, A.peak


def _rope_tables():
    rows = L // 64
    row_pos = np.repeat(np.arange(rows), 64).astype(np.float32)
    col_pos = np.tile(np.arange(64), rows).astype(np.float32)
    inv_freq = (np.float32(10000.0) ** (-np.arange(0, 64, 2, dtype=np.float32) / np.float32(64))).astype(np.float32)
    ang_r = row_pos[:, None] * inv_freq[None, :]
    ang_c = col_pos[:, None] * inv_freq[None, :]
    ang = np.concatenate([ang_r, ang_r, ang_c, ang_c], axis=1)
    cos = np.cos(ang).astype(np.float32).T
    sin = np.sin(ang).astype(np.float32).T
    return np.ascontiguousarray(np.stack([cos, sin], axis=1))


def _cmat():
    s = np.arange(128)[:, None]
    l = np.arange(128)[None, :]
    ident = np.eye(128, dtype=np.float32)
    triu = (s <= l).astype(np.float32)
    tril = (s >= l).astype(np.float32)
    negf = np.where(l >= s, 0.0, -30000.0).astype(np.float32)
    negb = np.where(l <= s, 0.0, -30000.0).astype(np.float32)
    ones = np.ones((128, 128), np.float32)
    rot = np.zeros((128, 128), np.float32)
    for m in range(128):
        if (m % 64) < 32:
            rot[m + 32, m] = -1.0
        else:
            rot[m - 32, m] = 1.0
    return np.ascontiguousarray(np.concatenate([ident, triu, tril, negf, negb, ones, rot], axis=1))


def prep_shared(inp, depth=DEPTH):
    f = lambda a: np.asarray(a, dtype=np.float32)
    out = {}
    for i, (gu, dn) in enumerate([("ffn1_w_gu", "ffn1_w_down"), ("ffn2_w_gu", "ffn2_w_down")]):
        w = f(inp[gu])[:depth].reshape(depth, 16, 128, 2, NJ, 128)
        out["wgu%d" % (i + 1)] = np.ascontiguousarray(w.transpose(0, 4, 2, 1, 3, 5)).reshape(depth, NJ, 128, 4096)
        w = f(inp[dn])[:depth].reshape(depth, NJ, 128, 16, 128)
        out["wd%d" % (i + 1)] = np.ascontiguousarray(w.transpose(0, 3, 2, 1, 4)).reshape(depth, 16, 128, DFF)
    w = f(inp["w_in"])[:depth].reshape(depth, 16, 128, 4128).transpose(0, 2, 1, 3)
    parts = [np.ascontiguousarray(w[:, :, :, c0:c0 + n]).reshape(depth, 128, 16 * n) for (_, c0, n) in WIN_BLOCKS]
    out["win"] = np.ascontiguousarray(np.concatenate(parts, axis=2))
    w = f(inp["w_out"])[:depth].reshape(depth, 16, 128, 16, 128)
    out["wout"] = np.ascontiguousarray(w.transpose(0, 3, 2, 1, 4)).reshape(depth, 16, 128, 2048)
    pcol = np.zeros((128, depth * NPC_L + 16), np.float32)
    prow = np.zeros((1, depth * NPR_L), np.float32)
    colv = lambda v: f(v).reshape(-1, 128).T
    for l in range(depth):
        b = l * NPC_L
        pcol[:, b:b + 16] = colv(inp["ffn1_norm"][l])
        pcol[:, b + 16:b + 32] = colv(inp["mix_norm"][l])
        pcol[:, b + 32:b + 48] = colv(inp["ffn2_norm"][l])
        pcol[:, b + 48:b + 56] = colv(inp["attn_out_norm"][l])
        pcol[:, b + 56] = f(inp["q_norm"][l])
        pcol[:, b + 57] = f(inp["k_norm"][l])
        cw = f(inp["conv_w"][l])
        for bb in range(12):
            for k in range(5):
                pcol[:, b + 58 + bb * 5 + k] = cw[k, bb * 128:(bb + 1) * 128]
        pcol[:, b + 118:b + 130] = colv(inp["conv_b"][l])
        r = l * NPR_L
        prow[0, r:r + 1024] = f(inp["ssd_out_norm"][l])
        prow[0, r + 1024:r + 1056] = f(inp["dt_bias"][l]).reshape(-1)
        prow[0, r + 1056:r + 1088] = f(inp["a_log"][l]).reshape(-1)
        prow[0, r + 1088:r + 1104] = f(inp["d_skip"][l])
        prow[0, r + 1104:r + 1232] = f(inp["q_norm"][l])
        prow[0, r + 1232:r + 1360] = f(inp["k_norm"][l])
    pcol[:, depth * NPC_L:depth * NPC_L + 16] = colv(inp["final_norm"])
    out["pcol"] = pcol
    out["prow"] = prow
    out["cmat"] = _cmat()
    out["rope"] = _rope_tables()
    return out


_CACHE = {}


def kernel(**inputs):
    x = np.asarray(inputs["x"], dtype=np.float32)
    shared = prep_shared(inputs)
    if "nc" not in _CACHE:
        _CACHE["nc"] = build()[0]
    nc = _CACHE["nc"]
    in_maps = []
    for c in range(NCORES):
        m = dict(shared)
        m["x"] = np.ascontiguousarray(x[c * 2:(c + 1) * 2])
        in_maps.append(m)
    res = run_bass_kernel_spmd(nc, in_maps, core_ids=list(range(NCORES)))
    return np.concatenate([r["out"] for r in res.results], axis=0).astype(np.float32)
```
